# Optimizing a Trainium2 kernel written in Bass

```python
import jax, jax.numpy as jnp
from jax import lax
import numpy as np

D_MODEL = 2048
BATCH = 8
SEQ = 2048
DEPTH = 1
DEC_BATCH = 128
DEC_SEQ = 8
PAST_LEN = 16384
PAGE_SIZE = 128

D_CONV = D_MODEL // 2
CONV_W = 3
HEAD_DIM = 64
D_ATTN = D_MODEL // 2
N_HEADS = D_ATTN // HEAD_DIM
N_KV_HEADS = N_HEADS // 4
GROUP = N_HEADS // N_KV_HEADS
D_KV = N_KV_HEADS * HEAD_DIM
WINDOW = 128
BLOCK = WINDOW
LN_EPS = 1e-5
ALPHA = (2 * DEPTH) ** 0.25
BETA = (8 * DEPTH) ** -0.25
NEG = -1e30
SPLIT_SIZES = (D_CONV, D_CONV, D_CONV, D_CONV, D_ATTN, D_KV, D_KV, D_ATTN, D_MODEL, D_MODEL)
D_IN_TOTAL = sum(SPLIT_SIZES)
SPLIT_OFFSETS = tuple(int(o) for o in np.cumsum(SPLIT_SIZES)[:-1])

kernel_name = "hybrid_shortconv_swa_sink_alibi_deepnorm_step"


def layer_norm(x, g, b):
    xf = x.astype(jnp.float32)
    mu = jnp.mean(xf, axis=-1, keepdims=True)
    var = jnp.mean(jnp.square(xf - mu), axis=-1, keepdims=True)
    y = (xf - mu) * lax.rsqrt(var + LN_EPS) * g.astype(jnp.float32) + b.astype(jnp.float32)
    return y.astype(x.dtype)


def alibi_slopes():
    h = jnp.arange(1, N_HEADS + 1, dtype=jnp.float32)
    return (2.0 ** (-8.0 * h / N_HEADS)).reshape(N_KV_HEADS, GROUP)


def short_conv(u, prev, conv_w):
    L = u.shape[1]
    u_pad = jnp.concatenate([prev.astype(u.dtype), u], axis=1)
    y = conv_w[0] * u_pad[:, 0:L]
    for tap in range(1, CONV_W):
        y = y + conv_w[tap] * u_pad[:, tap:tap + L]
    return y, u_pad[:, -(CONV_W - 1):]


def sink_window_attend(q, k, v, delta, valid, sinks):
    s = jnp.einsum('...qkgd,...skd->...kgqs', q.astype(jnp.float32), k.astype(jnp.float32)) * (HEAD_DIM ** -0.5)
    s = s - alibi_slopes()[:, :, None, None] * delta
    s = jnp.where(valid, s, NEG)
    sink = sinks.astype(jnp.float32).reshape(N_KV_HEADS, GROUP)[:, :, None, None]
    m = jnp.maximum(jnp.max(s, axis=-1, keepdims=True), sink)
    p = jnp.exp(s - m)
    denom = jnp.sum(p, axis=-1, keepdims=True) + jnp.exp(sink - m)
    o = jnp.einsum('...kgqs,...skd->...qkgd', p / denom, v.astype(jnp.float32))
    return o.astype(q.dtype)


def prompt_window_attention(q, k, v, sinks):
    b, L = q.shape[0], q.shape[1]
    nb = L // BLOCK
    qb = q.reshape(b, nb, BLOCK, N_KV_HEADS, GROUP, HEAD_DIM)
    kb = k.reshape(b, nb, BLOCK, N_KV_HEADS, HEAD_DIM)
    vb = v.reshape(b, nb, BLOCK, N_KV_HEADS, HEAD_DIM)

    def band(t):
        prev = jnp.concatenate([jnp.zeros_like(t[:, :1]), t[:, :-1]], axis=1)
        return jnp.concatenate([prev, t], axis=2)

    i = jnp.arange(BLOCK)[:, None]
    j = jnp.arange(2 * BLOCK)[None, :]
    delta = BLOCK + i - j
    blk = jnp.arange(nb)[:, None, None]
    valid = (delta >= 0) & (delta <= WINDOW) & ((blk > 0) | (j >= BLOCK))
    o = sink_window_attend(qb, band(kb), band(vb), delta.astype(jnp.float32),
                           valid[:, None, None], sinks)
    return o.reshape(b, L, D_ATTN), k[:, -WINDOW:], v[:, -WINDOW:]


def sample_window_attention(q, k, v, sinks, cache_k, cache_v):
    b, lq = q.shape[0], q.shape[1]
    k_all = jnp.concatenate([cache_k.astype(k.dtype), k], axis=1)
    v_all = jnp.concatenate([cache_v.astype(v.dtype), v], axis=1)
    i = jnp.arange(lq)[:, None]
    j = jnp.arange(WINDOW + lq)[None, :]
    delta = WINDOW + i - j
    valid = (delta >= 0) & (delta <= WINDOW)
    o = sink_window_attend(q, k_all, v_all, delta.astype(jnp.float32), valid, sinks)
    return o.reshape(b, lq, D_ATTN), k_all[:, -WINDOW:], v_all[:, -WINDOW:]


def hybrid_layer(x, conv_prev, attend_fn, w_in, conv_w, w_conv_out, w_attn_out, w_out, ln_g, ln_b):
    bsz, L, _ = x.shape
    proj = jnp.einsum('bld,de->ble', x, w_in)
    gb, gc, h, z_c, q, k, v, z_a, gate_c, gate_a = jnp.split(proj, SPLIT_OFFSETS, axis=-1)
    conv_out, conv_state = short_conv(gc * h, conv_prev, conv_w)
    y_c = jax.nn.silu(z_c) * gb * conv_out
    q = q.reshape(bsz, L, N_KV_HEADS, GROUP, HEAD_DIM)
    k = k.reshape(bsz, L, N_KV_HEADS, HEAD_DIM)
    v = v.reshape(bsz, L, N_KV_HEADS, HEAD_DIM)
    o, k_state, v_state = attend_fn(q, k, v)
    y_a = jax.nn.silu(z_a) * o
    merged = (jax.nn.sigmoid(gate_c) * jnp.einsum('blc,cd->bld', y_c, w_conv_out)
              + jax.nn.sigmoid(gate_a) * jnp.einsum('bla,ad->bld', y_a, w_attn_out))
    out = jnp.einsum('bld,de->ble', merged, w_out)
    y = layer_norm(ALPHA * x + out, ln_g, ln_b)
    return y, k_state, v_state, conv_state


def setup_inputs(seed: int = 0) -> dict:
    key = jax.random.key(seed)
    ks = jax.random.split(key, 16)
    f32 = jnp.float32
    col_scale = jnp.asarray(np.concatenate([
        np.full(D_CONV, 1.0), np.full(D_CONV, 1.0), np.full(D_CONV, BETA), np.full(D_CONV, 1.0),
        np.full(D_ATTN, 1.0), np.full(D_KV, 1.0), np.full(D_KV, BETA), np.full(D_ATTN, 1.0),
        np.full(D_MODEL, 1.0), np.full(D_MODEL, 1.0)]).astype(np.float32))
    x_prompt = jax.random.normal(ks[0], (BATCH, SEQ, D_MODEL), f32)
    x_sample = jax.random.normal(ks[1], (DEC_BATCH, DEC_SEQ, D_MODEL), f32)
    cache_k = jax.random.normal(ks[2], (DEPTH, DEC_BATCH, WINDOW, N_KV_HEADS, HEAD_DIM), f32)
    cache_v = jax.random.normal(ks[3], (DEPTH, DEC_BATCH, WINDOW, N_KV_HEADS, HEAD_DIM), f32) * BETA
    state_conv = jax.random.normal(ks[4], (DEPTH, DEC_BATCH, CONV_W - 1, D_CONV), f32) * BETA
    w_in = jax.random.normal(ks[5], (DEPTH, D_MODEL, D_IN_TOTAL), f32) * (D_MODEL ** -0.5) * col_scale
    conv_w = jax.random.normal(ks[6], (DEPTH, CONV_W, D_CONV), f32) * (CONV_W ** -0.5)
    attn_sinks = jax.random.normal(ks[7], (DEPTH, N_HEADS), f32) * 0.5
    w_conv_out = jax.random.normal(ks[8], (DEPTH, D_CONV, D_MODEL), f32) * (D_CONV ** -0.5) * BETA
    w_attn_out = jax.random.normal(ks[9], (DEPTH, D_ATTN, D_MODEL), f32) * (D_ATTN ** -0.5) * BETA
    w_out = jax.random.normal(ks[10], (DEPTH, D_MODEL, D_MODEL), f32) * (D_MODEL ** -0.5) * BETA
    ln_g = 1.0 + 0.02 * jax.random.normal(ks[11], (DEPTH, D_MODEL), f32)
    ln_b = 0.02 * jax.random.normal(ks[12], (DEPTH, D_MODEL), f32)
    return {"x_prompt": x_prompt, "x_sample": x_sample, "cache_k": cache_k, "cache_v": cache_v,
            "state_conv": state_conv, "w_in": w_in, "conv_w": conv_w, "attn_sinks": attn_sinks,
            "w_conv_out": w_conv_out, "w_attn_out": w_attn_out, "w_out": w_out,
            "ln_g": ln_g, "ln_b": ln_b}


def reference(x_prompt, x_sample, cache_k, cache_v, state_conv, w_in, conv_w, attn_sinks,
              w_conv_out, w_attn_out, w_out, ln_g, ln_b):
    xp, xs = x_prompt, x_sample
    kp_l, vp_l, cp_l, ks_l, vs_l, cs_l = [], [], [], [], [], []
    for l in range(DEPTH):
        sinks = attn_sinks[l]
        conv_zero = jnp.zeros((xp.shape[0], CONV_W - 1, D_CONV), xp.dtype)
        xp, kp, vp, cp = hybrid_layer(
            xp, conv_zero,
            lambda q, k, v, s_=sinks: prompt_window_attention(q, k, v, s_),
            w_in[l], conv_w[l], w_conv_out[l], w_attn_out[l], w_out[l], ln_g[l], ln_b[l])
        ck, cv = cache_k[l], cache_v[l]
        xs, k_s, v_s, c_s = hybrid_layer(
            xs, state_conv[l],
            lambda q, k, v, s_=sinks, ck_=ck, cv_=cv: sample_window_attention(q, k, v, s_, ck_, cv_),
            w_in[l], conv_w[l], w_conv_out[l], w_attn_out[l], w_out[l], ln_g[l], ln_b[l])
        kp_l.append(kp); vp_l.append(vp); cp_l.append(cp)
        ks_l.append(k_s); vs_l.append(v_s); cs_l.append(c_s)
    k_prompt = jnp.stack(kp_l)
    v_prompt = jnp.stack(vp_l)
    conv_prompt = jnp.stack(cp_l)
    k_sample = jnp.stack(ks_l)
    v_sample = jnp.stack(vs_l)
    conv_sample = jnp.stack(cs_l)
    return (xp, xs, k_prompt, v_prompt, conv_prompt, k_sample, v_sample, conv_sample)
```

```python
import os
import numpy as np
from contextlib import ExitStack
import concourse.bass as bass
import concourse.mybir as mybir
from concourse.bass_utils import run_bass_kernel_spmd

F32, BF16 = mybir.dt.float32, mybir.dt.bfloat16
AF = mybir.ActivationFunctionType
ALU = mybir.AluOpType

D = 2048; DC = 1024; NH = 16; NKV = 4; HD = 64; WIN = 128
NCORES = 8; SEQ = 2048; NSEQ = 16; DS = 8
NTOK = SEQ + NSEQ * DS
TMAX = 1152
ALPHA = float(2.0 ** 0.25)
LN_EPS = 1e-5
BLOCKS = [(list(range(0, 8)), True), (list(range(8, 16)), False)]
NDMASEM = 40


class _Stop(Exception):
    pass


def _chk(name):
    if os.environ.get("KSTOP", "") == name:
        raise _Stop()


class Trk:
    __slots__ = ("w", "r", "excl", "bank_id")

    def __init__(self, excl=False, bank_id=None):
        self.w = None
        self.r = {}
        self.bank_id = bank_id
        self.excl = excl


class Stream:
    def __init__(self, name, sem):
        self.name, self.sem, self.n, self.ops, self.seen = name, sem, 0, [], {}

    def add(self, fn, deps, inc):
        waits = []
        for d in deps:
            if d is None:
                continue
            sem, val, key = d
            if key == "pe" and self.name == "pe":
                continue
            if self.seen.get(key, 0) >= val:
                continue
            self.seen[key] = val
            waits.append((sem, val))
        self.ops.append((waits, fn, inc))

    def emit(self, eng):
        for waits, fn, inc in self.ops:
            for sem, val in waits:
                eng.wait_ge(sem, val)
            ins = fn(eng)
            if inc is not None:
                ins.then_inc(inc[0], inc[1])


class K:
    def __init__(self, nc, es):
        self.nc, self.es = nc, es
        self.st = {}
        for nm in ("pe", "act", "dve", "pool", "sp"):
            self.st[nm] = Stream(nm, es.enter_context(nc.semaphore("s_" + nm)))
        self.dsems = {"sp": [es.enter_context(nc.semaphore("d%d" % i)) for i in range(NDMASEM)],
                      "pool": [es.enter_context(nc.semaphore("g%d" % i)) for i in range(16)],
                      "act": [es.enter_context(nc.semaphore("a%d" % i)) for i in range(6)]}
        self.ndma = {"sp": 0, "pool": 0, "act": 0}
        self.final = []
        self.bank_rr = 0
        self.reserved = set()
        self.held = set()

    def deps(self, sname, reads, writes, extra=()):
        deps = list(extra)
        for t in reads:
            deps.append(t.w)
            if t.excl:
                for key, tok in t.r.items():
                    if key != sname:
                        deps.append(tok)
        for t in writes:
            deps.append(t.w)
            for key, tok in t.r.items():
                deps.append(tok)
        return deps

    def mark(self, tok, rkey, reads, writes):
        for t in reads:
            t.r[rkey] = tok
        for t in writes:
            t.w = tok
            t.r = {}

    def op(self, sname, fn, reads=(), writes=(), extra=(), hold=()):
        for t in reads:
            if t.bank_id is not None and t.bank_id not in hold:
                self.held.discard(t.bank_id)
        s = self.st[sname]
        deps = self.deps(sname, reads, writes, extra)
        s.n += 1
        tok = (s.sem, s.n, sname)
        s.add(fn, deps, (s.sem, 1))
        self.mark(tok, sname, reads, writes)
        return tok

    def group(self, fns, reads=(), writes=(), extra=()):
        s = self.st["pe"]
        deps = self.deps("pe", reads, writes, extra)
        s.n += 1
        tok = (s.sem, s.n, "pe")
        for i, fn in enumerate(fns):
            s.add(fn, deps if i == 0 else (), (s.sem, 1) if i == len(fns) - 1 else None)
        self.mark(tok, "pe", reads, writes)
        return tok

    def dma(self, sname, out, in_, reads=(), writes=(), extra=(), final=False):
        s = self.st[sname]
        i = self.ndma[sname]
        self.ndma[sname] += 1
        pool_ = self.dsems[sname]
        sem = pool_[i % len(pool_)]
        val = 16 * (i // len(pool_) + 1)
        key = "dma_%s%d" % (sname, i % len(pool_))
        deps = self.deps(key, reads, writes, extra)
        if val > 16:
            deps.append((sem, val - 16, key))
        s.add(lambda e, o=out, n=in_: e.dma_start(out=o, in_=n), deps, (sem, 16))
        tok = (sem, val, key)
        self.mark(tok, key, reads, writes)
        if final:
            self.final.append(tok)
        return tok

    def nb(self):
        for _ in range(16):
            b = self.bank_rr % 8
            self.bank_rr += 1
            if b not in self.reserved and b not in self.held:
                self.held.add(b)
                return b
        raise RuntimeError("out of PSUM banks: held=%s reserved=%s" % (self.held, self.reserved))


def _mm(out, lhsT, rhs, start, stop, tp=None):
    if tp is None:
        return lambda e: e.matmul(out, lhsT=lhsT, rhs=rhs, start=start, stop=stop)
    return lambda e: e.matmul(out, lhsT=lhsT, rhs=rhs, start=start, stop=stop, tile_position=tp)


def _tr(out, in_, ident):
    return lambda e: e.transpose(out, in_, ident)


def _act(out, in_, func, scale=1.0, bias=0.0):
    return lambda e: e.activation(out=out, in_=in_, func=func, bias=bias, scale=scale)


def _tt(out, in0, in1, op):
    return lambda e: e.tensor_tensor(out=out, in0=in0, in1=in1, op=op)


def _stt(out, in0, scalar, in1, op0, op1):
    return lambda e: e.scalar_tensor_tensor(out=out, in0=in0, scalar=scalar, in1=in1, op0=op0, op1=op1)


def _ts(out, in0, s1, op0):
    return lambda e: e.tensor_scalar(out=out, in0=in0, scalar1=s1, scalar2=None, op0=op0)


def _cp(out, in_):
    return lambda e: e.tensor_copy(out=out, in_=in_)


def _ms(ap, v):
    return lambda e: e.memset(ap, v)


def build_nc():
    nc = bass.Bass("TRN2", target_bir_lowering=False)

    def din(name, shape):
        return nc.dram_tensor(name, list(shape), F32, kind="ExternalInput").ap()

    def dout(name, shape):
        return nc.dram_tensor(name, list(shape), F32, kind="ExternalOutput").ap()

    x_d = din("x", [NTOK, D])
    ck_d = din("ck", [NSEQ, WIN, 256])
    cv_d = din("cv", [NSEQ, WIN, 256])
    sc_d = din("sc", [NSEQ * 2, DC])
    wa_d = din("wa", [52, 128, 16 * 128])
    wb_d = din("wb", [16, 128, 48 * 128])
    wc_d = din("wc", [4, 128, 16 * 512])
    cw_d = din("cw", [128, 24])
    sink_d = din("sinks", [128, 16])
    g_d = din("lng", [128, D])
    b_d = din("lnb", [128, D])
    id_d = din("ident", [128, 128])
    mp_d = din("mprev", [128, NH * 128])
    mo_d = din("mown", [128, NH * 128])
    msn_d = din("msnew", [128, NH * 128])
    msc_d = din("mscache", [128, NH * 8])

    y_d = dout("y", [NTOK, D])
    kvp_d = dout("kvp", [128, 512])
    cst_d = dout("cst", [34, DC])
    ks_d = dout("ks", [NSEQ, WIN, 256])
    vs_d = dout("vs", [NSEQ, WIN, 256])

    with ExitStack() as es:
        k = K(nc, es)

        def sb(name, shape, dt):
            return es.enter_context(nc.sbuf_tensor(name, list(shape), dt))

        big1 = sb("big1", [128, 36864], BF16)
        big2 = sb("big2", [128, 18432], BF16)
        wring = sb("wring", [128, 2, 8192], BF16)
        mg = sb("mg", [128, 2, D], F32)
        mtab = mg[:, :, :].rearrange("p k (h q) -> p k h q", h=NH)
        gb = mg
        T_mg = Trk()
        identf = sb("identf", [128, 128], F32)
        identb = sb("identb", [128, 128], BF16)
        esink = sb("esink", [128, 16], F32)
        cwt = sb("cwt", [128, 8, 3], F32)
        kprevA = sb("kprevA", [128, 2, 128], BF16)
        kprevB = sb("kprevB", [128, 2, 128], BF16)
        vprev = sb("vprev", [128, 4, 65], BF16)
        uprev = sb("uprev", [128, 8, 2], F32)
        csall = sb("csall", [128, 8, 34], F32)
        scT = sb("scT", [128, 8, 32], F32)
        NSLOT = 23
        scr = sb("scr", [128, NSLOT, 1024], BF16)
        ps = es.enter_context(nc.psum_tensor("ps", [128, 8, 512], F32))

        xT = big1[:, 0:16 * TMAX].rearrange("p (k t) -> p k t", k=16)
        ycT = big1[:, 16 * TMAX:24 * TMAX].rearrange("p (k t) -> p k t", k=8)
        yaT = big1[:, 24 * TMAX:32 * TMAX].rearrange("p (k t) -> p k t", k=8)
        pre = big1[:, :].bitcast(F32).rearrange("p (t d) -> p t d", d=D)
        mT = big2[:, 0:16 * TMAX].rearrange("p (k t) -> p k t", k=16)
        qT = big2[:, 0:8 * TMAX].rearrange("p (k t) -> p k t", k=8)
        kTA = big2[:, 8 * TMAX:10 * TMAX].rearrange("p (k t) -> p k t", k=2)
        kTB = big2[:, 10 * TMAX:12 * TMAX].rearrange("p (k t) -> p k t", k=2)
        vaug = big2[:, 12 * TMAX:12 * TMAX + 9 * 260].rearrange("p (t g d) -> p t g d", t=9, g=4)

        ycs = big1[:, 16 * TMAX:24 * TMAX]
        msnew_v = ycs[:, 0:4096].bitcast(F32)
        msc_v = ycs[:, 4096:4352].bitcast(F32)
        Ppad_flat = ycs[:, 4352:6400]
        cvaug_flat = ycs[:, 6400:7440]
        T_ys_m, T_ys_p, T_ys_v = Trk(), Trk(), Trk()
        T_ys_extra = []

        def sl_f32(s0, n):
            return scr[:, s0, :].bitcast(F32) if n == 512 else None

        def scr_f32(s0, nslots):
            return scr[:, s0:s0 + nslots, :].rearrange("p s e -> p (s e)").bitcast(F32)

        def scr_bf(s0, nslots):
            return scr[:, s0:s0 + nslots, :].rearrange("p s e -> p (s e)")

        T_scr = [Trk() for _ in range(NSLOT)]

        def ts(s0, n=1):
            return T_scr[s0:s0 + n]

        T_bank = [Trk(excl=True, bank_id=i) for i in range(8)]
        T_w = [Trk(), Trk()]
        T_wb = [[Trk() for _ in range(4)] for _ in range(2)]
        T_const = Trk()
        T_xT = [Trk() for _ in range(3)]
        T_q = [[Trk() for _ in range(3)] for _ in range(8)]
        T_kT = [[Trk() for _ in range(3)] for _ in range(2)]
        T_v = [Trk() for _ in range(9)]
        T_yc = [[Trk() for _ in range(3)] for _ in range(8)]
        T_ya = [[Trk() for _ in range(9)] for _ in range(8)]
        T_m = [[Trk() for _ in range(3)] for _ in range(16)]
        T_pre = [[Trk() for _ in range(4)] for _ in range(9)]
        T_kprev = Trk()
        T_uprev = [Trk() for _ in range(8)]
        T_cs = [Trk() for _ in range(8)]
        G1 = T_xT + sum(T_yc, []) + sum(T_ya, []) + sum(T_pre, [])
        G2 = sum(T_q, []) + sum(T_kT, []) + T_v + sum(T_m, [])

        def bank(b):
            return ps[:, b, :]

        def bankb(b):
            return ps[:, b, :].bitcast(BF16)

        cl = []
        cl.append(k.dma("sp", identf[:], id_d[:, :], writes=[T_const]))
        cl.append(k.dma("pool", identb[:], id_d[:, :], writes=[Trk()]))
        cl.append(k.dma("sp", esink[:], sink_d[:, :], writes=[Trk()]))
        cl.append(k.dma("sp", cwt[:], cw_d.rearrange("p (c t) -> p c t", c=8), writes=[Trk()]))
        scrows = scr_f32(0, 2)[0:32, :]
        cl.append(k.dma("sp", scrows, sc_d[:, :], writes=ts(0, 2)))
        CT = Trk()
        CT.w = None
        consts_ready = cl
        t_es = k.op("act", _act(esink[:], esink[:], AF.Exp), extra=consts_ready, writes=[T_const])
        b0 = k.nb()
        fns = [_tr(bank(b0)[:, c * 32:(c + 1) * 32], scrows[:, c * 128:(c + 1) * 128], identf[0:32, 0:32]) for c in range(8)]
        k.group(fns, reads=ts(0, 2) + [T_const], writes=[T_bank[b0]], extra=consts_ready)
        k.op("dve", _cp(scT[:], bank(b0)[:, 0:256].rearrange("p (c r) -> p c r", c=8)),
             reads=[T_bank[b0]], writes=[T_const])
        k.op("dve", _ms(uprev[:], 0.0), writes=T_uprev)

        k.dma("sp", msnew_v, msn_d[:, :], writes=[T_ys_m] + sum(T_yc, []))
        k.dma("sp", msc_v, msc_d[:, :], writes=[T_ys_m])
        lnq = []

        def fence_deps(trks):
            d = []
            for t in trks:
                d.append(t.w)
                d.extend(t.r.values())
            return d

        wdma_i = [0]

        def wload(src_ap, nelem, nb4=0, extra=()):
            s = wdma_i[0] % 2
            wdma_i[0] += 1
            dst = wring[:, s, 0:nelem]
            if nb4:
                dst = dst.rearrange("p (b e) -> p b e", b=nb4)
            k.dma("pool", dst, src_ap, writes=[T_w[s]] + T_wb[s], extra=extra)
            return s

        def wload_split(blk0):
            s = wdma_i[0] % 2
            wdma_i[0] += 1
            for j in range(4):
                k.dma("pool", wring[:, s, j * 2048:(j + 1) * 2048], wa_d[blk0 + j], writes=[T_wb[s][j]] + ([T_w[s]] if j == 0 else []))
            return s

        row0 = {True: 0}

        try:
            for bi, (ptiles, has_s) in enumerate(BLOCKS if not os.environ.get("KSTOP") else BLOCKS[:1]):
                ntl = len(ptiles) + (1 if has_s else 0)
                T = ntl * 128
                TP = len(ptiles) * 128
                rows = [pt * 128 for pt in ptiles] + ([SEQ] if has_s else [])
                ntiles_n = [(o, min(512, T - o)) for o in range(0, T, 512)]
                nnt = len(ntiles_n)

                def nt_of(ti):
                    return ti // 4

                while any(t_ <= 4 for t_, _f in lnq):
                    lnq.pop(0)[1]()
                pre_slots = []
                XB4 = [9, 11, 13, 15, 17, 19, 21]
                NXB = len(XB4)
                xtoks = []
                if bi == 0:
                    pre_slots.append(wload_split(0))
                else:
                    for tj in range(min(NXB, ntl)):
                        sj = XB4[tj % NXB]
                        k.dma("pool", scr_bf(sj, 2), x_d[rows[tj]:rows[tj] + 128, :], writes=ts(sj, 2))
                for ti in range(ntl):
                    if bi == 0:
                        s = ti % 2
                        xsl = 2 * s
                        xb = scr_bf(xsl, 2)
                        stg = scr_f32(6 + 4 * s, 4)
                        xtoks.append(k.dma("sp", stg, x_d[rows[ti]:rows[ti] + 128, :], writes=ts(6 + 4 * s, 4)))
                        if ti == ntl - 3:
                            pre_slots.append(wload(wa_d[4:8].rearrange("b p e -> p b e"), 8192, 4, extra=xtoks[-1:]))
                        if ti % 2 == 0:
                            k.op("dve", _cp(xb, stg), reads=ts(6 + 4 * s, 4), writes=ts(xsl, 2))
                        else:
                            k.op("act", _act(xb, stg, AF.Copy), reads=ts(6 + 4 * s, 4), writes=ts(xsl, 2))
                    else:
                        xsl = XB4[ti % NXB]
                        xb = scr_bf(xsl, 2)
                    for hf in range(2):
                        b = k.nb()
                        fns = [_tr(bankb(b)[:, j * 128:(j + 1) * 128], xb[:, (hf * 8 + j) * 128:(hf * 8 + j + 1) * 128], identb[:])
                               for j in range(8)]
                        k.group(fns, reads=ts(xsl, 2) + [T_const], writes=[T_bank[b]], extra=consts_ready)
                        eng = "act" if (hf == 0 or bi > 0) else "dve"
                        src = bankb(b).rearrange("p (j t) -> p j t", j=8)
                        dst = xT[:, hf * 8:(hf + 1) * 8, ti * 128:(ti + 1) * 128]
                        first = (ti == 0 and hf == 0)
                        wr = [T_xT[nt_of(ti)]] + (T_xT if first else [])
                        ex = fence_deps(sum(T_pre[0:5], [])) if first else []
                        if eng == "act":
                            k.op("act", _act(dst, src, AF.Copy), reads=[T_bank[b]], writes=wr, extra=ex)
                        else:
                            k.op("dve", _cp(dst, src), reads=[T_bank[b]], writes=wr, extra=ex)
                        if lnq and (2 * ti + hf) % 3 == 2:
                            lnq.pop(0)[1]()
                    if bi > 0:
                        tj = ti + NXB
                        if tj < ntl:
                            sj = XB4[tj % NXB]
                            k.dma("pool", scr_bf(sj, 2), x_d[rows[tj]:rows[tj] + 128, :], writes=ts(sj, 2))
                        if ti == 0:
                            pre_slots.append(wload_split(0))
                        if ti == 1:
                            pre_slots.append(wload(wa_d[4:8].rearrange("b p e -> p b e"), 8192, 4))

                _chk("s0")
                def proj_group(slot, blk, nti, nkc=16, rhsT=None, kbase=0):
                    o, n = ntiles_n[nti]
                    wv = wring[:, slot, :].rearrange("p (b k j) -> p b k j", k=16, j=128)
                    b = k.nb()
                    fns = [_mm(bank(b)[:, 0:n], wv[:, blk, kc, :], xT[:, kc, o:o + n], kc == 0, kc == 15) for kc in range(16)]
                    k.group(fns, reads=[T_w[slot], T_wb[slot][blk], T_xT[nti]], writes=[T_bank[b]])
                    return b, o, n

                evac_rr = [0]

                def evac_copy(dst, src, reads, writes):
                    if lnq:
                        lnq.pop(0)[1]()
                    evac_rr[0] += 1
                    if evac_rr[0] % 2 == 0:
                        return k.op("act", _act(dst, src, AF.Copy), reads=reads, writes=writes)
                    return k.op("dve", _cp(dst, src), reads=reads, writes=writes)

                for qh in range(2):
                    slot = pre_slots[qh]
                    for bq in range(4):
                        c = qh * 4 + bq
                        for nti in range(nnt):
                            b, o, n = proj_group(slot, bq, nti)
                            wr = [T_q[c][nti]] + (G2 if (qh == 0 and bq == 0 and nti == 0) else [])
                            evac_copy(qT[:, c, o:o + n], bank(b)[:, 0:n], [T_bank[b]], wr)
                while lnq:
                    lnq.pop(0)[1]()
                k.dma("sp", mg[:, 0, :], mp_d[:, :], writes=[T_mg])
                k.dma("sp", mg[:, 1, :], mo_d[:, :], writes=[T_mg])
                _chk("a1q")
                slot = wload(wa_d[8:12].rearrange("b p e -> p b e"), 8192, 4)
                for j in range(2):
                    for nti in range(nnt):
                        b, o, n = proj_group(slot, j, nti)
                        k.op("act", _act(kTA[:, j, o:o + n], bank(b)[:, 0:n], AF.Copy), reads=[T_bank[b]], writes=[T_kT[j][nti]], hold=[b])
                        k.op("dve", _cp(kTB[0:64, j, o:o + n], bank(b)[64:128, 0:n]), reads=[T_bank[b]], writes=[T_kT[j][nti]], hold=[b])
                        k.op("act", _act(kTB[64:128, j, o:o + n], bank(b)[0:64, 0:n], AF.Copy), reads=[T_bank[b]], writes=[T_kT[j][nti]])
                _chk("a1k")
                wv = wring[:, slot, :].rearrange("p (b k j) -> p b k j", k=16, j=128)
                if "vms" not in os.environ.get("KSKIP", ""):
                    k.op("dve", _ms(vaug[:, 0:ntl, :, 64:65], 1.0), writes=T_v[0:ntl])
                for ti in range(ntl):
                    is_s = has_s and ti == ntl - 1
                    kout = is_s or (rows[ti] == SEQ - 128)
                    b = k.nb()
                    blist = [0, 1, 2, 3] if kout else [2, 3]
                    voff = 256 if kout else 0
                    fns = []
                    for jj, wb_ in enumerate(blist):
                        fns += [_mm(bank(b)[:, jj * 128:(jj + 1) * 128], xT[:, kc, ti * 128:(ti + 1) * 128], wv[:, wb_, kc, :], kc == 0, kc == 15)
                                for kc in range(16)]
                    k.group(fns, reads=[T_w[slot], T_xT[nt_of(ti)]], writes=[T_bank[b]])
                    if "vev" not in os.environ.get("KSKIP", ""):
                        k.op("act", _act(vaug[:, ti, :, 0:64], bank(b)[:, voff:voff + 256].rearrange("p (g d) -> p g d", g=4), AF.Copy),
                             reads=[T_bank[b]], writes=[T_v[ti]], hold=[b] if kout else [])
                    if kout and "vkv" not in os.environ.get("KSKIP", ""):
                        kvo = scr_f32(4, 1)
                        k.op("dve", _cp(kvo, bank(b)[:, :]), reads=[T_bank[b]], writes=ts(4, 1))
                        if is_s and "sout" in os.environ.get("KSKIP", ""):
                            pass
                        elif is_s:
                            for sq in range(NSEQ):
                                k.dma("sp", ks_d[sq, WIN - DS:WIN, :], kvo[sq * 8:(sq + 1) * 8, 0:256], reads=ts(4, 1), final=True)
                                k.dma("sp", vs_d[sq, WIN - DS:WIN, :], kvo[sq * 8:(sq + 1) * 8, 256:512], reads=ts(4, 1), final=True)
                        else:
                            k.dma("sp", kvp_d[:, :], kvo, reads=ts(4, 1), final=True)
                if has_s and "d2d" not in os.environ.get("KSKIP", ""):
                    k.dma("sp", ks_d[:, 0:WIN - DS, :], ck_d[:, DS:WIN, :], final=True)
                    k.dma("sp", vs_d[:, 0:WIN - DS, :], cv_d[:, DS:WIN, :], final=True)

                _chk("a1")
                E_sb = [scr_f32(15, 2), scr_f32(17, 2)]
                P_sb = [scr_bf(19, 1), scr_bf(20, 1)]
                o_sb = scr_bf(21, 1)
                den = scr_f32(22, 1)
                pb_rr = [0]
                T_P = [[Trk(), Trk()], [Trk(), Trk()]]

                def att_scores(ti, g, blks):
                    sbanks = []
                    for hp in range(2):
                        b = k.nb()
                        fns = []
                        rd = [T_kT[g // 2][nt_of(ti)]]
                        useA = (g % 2) == hp
                        for bi2, kind in enumerate(blks):
                            if kind == "own":
                                kt = (kTA if useA else kTB)[hp * 64:(hp + 1) * 64, g // 2, ti * 128:(ti + 1) * 128]
                            elif ti > 0:
                                kt = (kTA if useA else kTB)[hp * 64:(hp + 1) * 64, g // 2, (ti - 1) * 128:ti * 128]
                                rd.append(T_kT[g // 2][nt_of(ti - 1)])
                            else:
                                kt = (kprevA if useA else kprevB)[hp * 64:(hp + 1) * 64, g // 2, :]
                                rd.append(T_kprev)
                            for kk in range(2):
                                h = 4 * g + 2 * kk + hp
                                qt = qT[hp * 64:(hp + 1) * 64, h // 2, ti * 128:(ti + 1) * 128]
                                rd.append(T_q[h // 2][nt_of(ti)])
                                c0 = (bi2 * 2 + kk) * 128
                                fns.append(_mm(bank(b)[:, c0:c0 + 128], kt, qt, True, True))
                        k.group(fns, reads=rd, writes=[T_bank[b]])
                        sbanks.append(b)
                    return sbanks

                def att_blks(ti, is_s):
                    gti = rows[ti] // 128
                    return ["own"] if (is_s or gti == 0) else ["prev", "own"]

                def att_S(ti, g, is_s):
                    blks = att_blks(ti, is_s)
                    nbk = len(blks)
                    n_ = nbk * 256
                    sbanks = att_scores(ti, g, blks)
                    pb = pb_rr[0] % 2
                    pb_rr[0] += 1
                    Ev = E_sb[pb].rearrange("p (b e) -> p b e", b=2)
                    Pv = P_sb[pb].rearrange("p (b e) -> p b e", b=2)
                    for hp in range(2):
                        k.op("act", _act(Ev[:, hp, 0:n_], bank(sbanks[hp])[:, 0:n_], AF.Exp, scale=0.125),
                             reads=[T_bank[sbanks[hp]]], writes=ts(15 + 2 * pb + hp, 1))
                        h0 = 4 * g + hp
                        if is_s:
                            mt = msnew_v.rearrange("p (h q) -> p h q", h=NH)[:, h0:h0 + 3:2, :]
                            mrd = [T_ys_m]
                        elif nbk == 2:
                            mt = mtab[:, 0:2, h0:h0 + 3:2, :]
                            mrd = [T_mg]
                        else:
                            mt = mtab[:, 1, h0:h0 + 3:2, :]
                            mrd = [T_mg]
                        if nbk == 2:
                            pv_ = Pv[:, hp, 0:n_].rearrange("p (b h q) -> p b h q", b=2, h=2)
                            ev_ = Ev[:, hp, 0:n_].rearrange("p (b h q) -> p b h q", b=2, h=2)
                        else:
                            pv_ = Pv[:, hp, 0:n_].rearrange("p (h q) -> p h q", h=2)
                            ev_ = Ev[:, hp, 0:n_].rearrange("p (h q) -> p h q", h=2)
                        k.op("dve", _tt(pv_, ev_, mt, ALU.mult), reads=ts(15 + 2 * pb + hp, 1) + mrd, writes=[T_P[pb][hp]], extra=consts_ready)
                    return pb

                def att_PV(ti, g, is_s, pb):
                    blks = att_blks(ti, is_s)
                    nbk = len(blks)
                    Pv = P_sb[pb].rearrange("p (b e) -> p b e", b=2)
                    ob = (4 + g) if is_s else k.nb()
                    fns = []
                    rd = T_P[pb] + [T_v[ti]]
                    for hh in range(4):
                        kk, hp = hh // 2, hh % 2
                        for bi2, kind in enumerate(blks):
                            if kind == "own":
                                vv = vaug[:, ti, g, :]
                            elif ti > 0:
                                vv = vaug[:, ti - 1, g, :]
                                rd.append(T_v[ti - 1])
                            else:
                                vv = vprev[:, g, :]
                                rd.append(T_kprev)
                            c0 = (bi2 * 2 + kk) * 128
                            fns.append(_mm(bank(ob)[:, hh * 65:(hh + 1) * 65], Pv[:, hp, c0:c0 + 128], vv,
                                           (bi2 == 0 and hh == 0) if is_s else (bi2 == 0), (bi2 == nbk - 1) and not is_s))
                    k.group(fns, reads=rd, writes=[T_bank[ob]])
                    if not is_s:
                        att_finish_group(ti, g, ob)
                    return ob

                def att_tile(ti, is_s):
                    for g in range(4):
                        pb = att_S(ti, g, is_s)
                        att_PV(ti, g, is_s, pb)

                def att_gen(ntp):
                    for ti in range(ntp):
                        pbs = [None] * 4
                        pbs[0] = att_S(ti, 0, False)
                        yield
                        pbs[1] = att_S(ti, 1, False)
                        yield
                        att_PV(ti, 0, False, pbs[0])
                        pbs[2] = att_S(ti, 2, False)
                        yield
                        att_PV(ti, 1, False, pbs[1])
                        pbs[3] = att_S(ti, 3, False)
                        yield
                        att_PV(ti, 2, False, pbs[2])
                        yield
                        att_PV(ti, 3, False, pbs[3])
                        yield
                        att_transpose(ti)
                        yield

                def att_finish_group(ti, g, ob):
                    ov = bank(ob)[:, 0:260].rearrange("p (h d) -> p h d", h=4)
                    dn = den[:, g * 8:g * 8 + 4]
                    rdn = den[:, g * 8 + 4:g * 8 + 8]
                    k.op("dve", _tt(dn.rearrange("p (h o) -> p h o", o=1), ov[:, :, 64:65],
                                    esink[:, 4 * g:4 * g + 4].rearrange("p (h o) -> p h o", o=1), ALU.add),
                         reads=[T_bank[ob], T_const], writes=ts(22, 1), hold=[ob])
                    k.op("dve", lambda e, a=rdn, b_=dn: e.reciprocal(out=a, in_=b_), reads=ts(22, 1), writes=ts(22, 1))
                    osv = o_sb.rearrange("p (h d) -> p h d", h=NH)[:, 4 * g:4 * g + 4, :]
                    k.op("dve", _tt(osv, ov[:, :, 0:64], rdn.rearrange("p (h o) -> p h o", o=1).to_broadcast([128, 4, 64]), ALU.mult),
                         reads=[T_bank[ob]] + ts(22, 1), writes=ts(21, 1))

                def att_transpose(ti):
                    b = k.nb()
                    fns = [_tr(bankb(b)[:, c * 128:(c + 1) * 128], o_sb[:, c * 128:(c + 1) * 128], identb[:]) for c in range(8)]
                    k.group(fns, reads=ts(21, 1) + [T_const], writes=[T_bank[b]])
                    k.op("act", _act(yaT[:, :, ti * 128:(ti + 1) * 128], bankb(b).rearrange("p (c t) -> p c t", c=8), AF.Copy),
                         reads=[T_bank[b]], writes=[T_ya[c][ti] for c in range(8)], extra=fence_deps(sum(T_pre[6:9], [])))

                agen = att_gen(len(ptiles))
                adone = [False]

                def att_step():
                    if not adone[0]:
                        try:
                            next(agen)
                        except StopIteration:
                            adone[0] = True

                if os.environ.get("KNOINTER"):
                    while not adone[0]:
                        att_step()
                if has_s:
                    lt = len(ptiles) - 1
                    k.op("act", _act(kprevA[:], kTA[:, :, lt * 128:(lt + 1) * 128], AF.Copy), reads=[T_kT[0][nt_of(lt)], T_kT[1][nt_of(lt)]], writes=[T_kprev])
                    k.op("act", _act(kprevB[:], kTB[:, :, lt * 128:(lt + 1) * 128], AF.Copy), reads=[T_kT[0][nt_of(lt)], T_kT[1][nt_of(lt)]], writes=[T_kprev])
                    k.op("act", _act(vprev[:], vaug[:, lt, :, :], AF.Copy), reads=[T_v[lt]], writes=[T_kprev])
                    _chk("a2p")
                    ti = ntl - 1
                    k.reserved = {4, 5, 6, 7}
                    k.bank_rr = 0
                    msc = msc_v.rearrange("p (h q) -> p h q", h=NH)
                    SB2 = [(8, 9, 10, 11, 12), (0, 1, 2, 3, 5)]
                    cvaug2 = [cvaug_flat.rearrange("p (b g d) -> p b g d", b=4, g=4),
                              ycs[:, 7440:8480].rearrange("p (b g d) -> p b g d", b=4, g=4)]
                    T_cva = [T_ys_v, Trk()]
                    T_ys_extra.append(T_cva[1])
                    Ppad = Ppad_flat.rearrange("p (b h s) -> p b h s", b=4, h=NH)
                    k.op("dve", _ms(Ppad_flat, 0.0), writes=[T_ys_p], extra=[T_ys_m.w])
                    for i_ in range(2):
                        k.op("dve", _ms(cvaug2[i_][:, :, :, 64:65], 1.0), writes=[T_cva[i_]], extra=[T_ys_m.w])

                    def samp_A(sg):
                        sl = SB2[sg % 2]
                        ckg, cvg = scr_bf(sl[0], 1), scr_bf(sl[1], 1)
                        kcA = scr_bf(sl[2], 1).rearrange("p (b j s) -> p b j s", b=4, j=2)
                        kcB = scr_bf(sl[3], 1).rearrange("p (b j s) -> p b j s", b=4, j=2)
                        cva = cvaug2[sg % 2]
                        k.dma("pool", ckg.rearrange("p (b e) -> p b e", b=4), ck_d[sg * 4:(sg + 1) * 4].rearrange("b s e -> s b e"), writes=ts(sl[0], 1))
                        k.dma("pool", cvg.rearrange("p (b e) -> p b e", b=4), cv_d[sg * 4:(sg + 1) * 4].rearrange("b s e -> s b e"), writes=ts(sl[1], 1))
                        b = k.nb()
                        fns = [_tr(bankb(b)[:, (bl * 2 + j) * 128:(bl * 2 + j + 1) * 128], ckg[:, bl * 256 + j * 128:bl * 256 + (j + 1) * 128], identb[:])
                               for bl in range(4) for j in range(2)]
                        k.group(fns, reads=ts(sl[0], 1) + [T_const], writes=[T_bank[b]])
                        Tv = bankb(b).rearrange("p (b j s) -> p b j s", b=4, j=2)
                        k.op("act", _act(kcA[:], Tv[:], AF.Copy), reads=[T_bank[b]], writes=ts(sl[2], 1), hold=[b])
                        k.op("dve", _cp(kcB[0:64], Tv[64:128]), reads=[T_bank[b]], writes=ts(sl[3], 1), hold=[b])
                        k.op("act", _act(kcB[64:128], Tv[0:64], AF.Copy), reads=[T_bank[b]], writes=ts(sl[3], 1))
                        k.op("dve", _cp(cva[:, :, :, 0:64], cvg.rearrange("p (b g d) -> p b g d", b=4, g=4)), reads=ts(sl[1], 1), writes=[T_cva[sg % 2]])

                    def samp_B(sg):
                        sl = SB2[sg % 2]
                        kcA = scr_bf(sl[2], 1).rearrange("p (b j s) -> p b j s", b=4, j=2)
                        kcB = scr_bf(sl[3], 1).rearrange("p (b j s) -> p b j s", b=4, j=2)
                        Ec = scr_f32(sl[4], 1).rearrange("p (a b h q) -> p a b h q", a=2, b=4, h=8)
                        cva = cvaug2[sg % 2]
                        sbk = []
                        for hp in range(2):
                            b = k.nb()
                            Sv = bank(b)[:, 0:256].rearrange("p (b h q) -> p b h q", b=4, h=8)
                            fns = []
                            for bl in range(4):
                                sq = sg * 4 + bl
                                c0 = ti * 128 + sq * 8
                                for g in range(4):
                                    useA = (g % 2) == hp
                                    kt = (kcA if useA else kcB)[hp * 64:(hp + 1) * 64, bl, g // 2, :]
                                    qt = qT[hp * 64:(hp + 1) * 64, 2 * g:2 * g + 2, c0:c0 + 8]
                                    fns.append(_mm(Sv[:, bl, 2 * g:2 * g + 2, :], kt, qt, True, True))
                            k.group(fns, reads=ts(sl[2], 1) + ts(sl[3], 1) + [T_q[c][nt_of(ti)] for c in range(8)], writes=[T_bank[b]])
                            sbk.append(b)
                        for hp in range(2):
                            k.op("act", _act(Ec[:, hp], bank(sbk[hp])[:, 0:256].rearrange("p (b h q) -> p b h q", b=4, h=8), AF.Exp, scale=0.125),
                                 reads=[T_bank[sbk[hp]]], writes=ts(sl[4], 1))
                        for bl in range(4):
                            k.op("dve", _tt(Ppad[:, bl, :, bl * 8:(bl + 1) * 8].rearrange("p (a h) q -> p a h q", a=2), Ec[:, :, bl, :, :],
                                            msc.rearrange("p (a h) q -> p a h q", a=2), ALU.mult),
                                 reads=ts(sl[4], 1) + [T_ys_m], writes=[T_ys_p])
                        for g in range(4):
                            fns = []
                            for kk in range(2):
                                for hp in range(2):
                                    h = 4 * g + 2 * kk + hp
                                    hq = hp * 8 + g * 2 + kk
                                    hh = h - 4 * g
                                    for bl in range(4):
                                        fns.append(_mm(bank(4 + g)[32 * sg:32 * sg + 32, hh * 65:(hh + 1) * 65], Ppad[:, bl, hq, :], cva[:, bl, g, :],
                                                       False, (bl == 3 and kk == 1 and hp == 1), tp=(0, 32 * sg)))
                            k.group(fns, reads=[T_ys_p, T_cva[sg % 2]], writes=[T_bank[4 + g]])

                    samp_A(0)
                    samp_A(1)
                    att_tile(ti, True)
                    samp_B(0)
                    samp_A(2)
                    samp_B(1)
                    samp_A(3)
                    samp_B(2)
                    samp_B(3)
                    for g in range(4):
                        att_finish_group(ti, g, 4 + g)
                    k.reserved = set()
                    att_transpose(ti)

                _chk("a2")
                for c in range(8):
                    slot = wload(wa_d[12 + 4 * c:16 + 4 * c].rearrange("b p e -> p b e"), 8192, 4)
                    ub = c % 2
                    u = scr_f32(3 * ub, 3)[:, 0:2 + TP]
                    k.op("dve", _cp(u[:, 0:2], uprev[:, c, :]), reads=[T_uprev[c]], writes=ts(3 * ub, 3))
                    if has_s:
                        us = scr_f32(14, 1)[:, 0:160].rearrange("p (b t) -> p b t", b=NSEQ)
                        k.op("dve", _cp(us[:, :, 0:2], scT[:, c, :].rearrange("p (b r) -> p b r", b=NSEQ)), reads=[T_const], writes=ts(14, 1))
                    for nti in range(nnt):
                        o, n = ntiles_n[nti]
                        st_ = 6 + 4 * (nti % 2)
                        gcs, szs, t1, t2 = (scr_f32(st_ + i, 1)[:, 0:n] for i in range(4))
                        bgc, _, _ = proj_group(slot, 0, nti)
                        k.op("act", _act(gcs, bank(bgc)[:, 0:n], AF.Copy), reads=[T_bank[bgc]], writes=ts(st_, 1))
                        att_step()
                        bzc, _, _ = proj_group(slot, 3, nti)
                        k.op("act", _act(szs, bank(bzc)[:, 0:n], AF.Silu), reads=[T_bank[bzc]], writes=ts(st_ + 1, 1))
                        att_step()
                        bh, _, _ = proj_group(slot, 1, nti)
                        samp = has_s and o >= TP
                        if not samp:
                            uo = u[:, 2 + o:2 + o + n]
                            k.op("dve", _tt(uo, gcs, bank(bh)[:, 0:n], ALU.mult), reads=ts(st_, 1) + [T_bank[bh]], writes=ts(3 * ub, 3))
                            u0, u1, u2 = u[:, o:o + n], u[:, o + 1:o + 1 + n], uo
                            tt1 = t1
                            urd = ts(3 * ub, 3)
                        else:
                            uo = us[:, :, 2:10]
                            k.op("dve", _tt(uo, gcs.rearrange("p (b t) -> p b t", b=NSEQ), bank(bh)[:, 0:n].rearrange("p (b t) -> p b t", b=NSEQ), ALU.mult),
                                 reads=ts(st_, 1) + [T_bank[bh]], writes=ts(14, 1))
                            u0, u1, u2 = us[:, :, 0:8], us[:, :, 1:9], uo
                            tt1 = t1.rearrange("p (b t) -> p b t", b=NSEQ)
                            urd = ts(14, 1)
                        att_step()
                        bgb, _, _ = proj_group(slot, 2, nti)
                        k.op("dve", _tt(t2, szs, bank(bgb)[:, 0:n], ALU.mult), reads=ts(st_ + 1, 1) + [T_bank[bgb]], writes=ts(st_ + 3, 1))
                        att_step()
                        k.op("dve", _ts(tt1, u0, cwt[:, c, 0:1], ALU.mult), reads=urd + [T_const], writes=ts(st_ + 2, 1))
                        k.op("dve", _stt(tt1, u1, cwt[:, c, 1:2], tt1, ALU.mult, ALU.add), reads=urd + ts(st_ + 2, 1), writes=ts(st_ + 2, 1))
                        k.op("dve", _stt(tt1, u2, cwt[:, c, 2:3], tt1, ALU.mult, ALU.add), reads=urd + ts(st_ + 2, 1), writes=ts(st_ + 2, 1))
                        k.op("dve", _tt(ycT[:, c, o:o + n], t1, t2, ALU.mult), reads=ts(st_ + 2, 2), writes=[T_yc[c][nti], T_ys_m, T_ys_p, T_ys_v] + T_ys_extra, extra=fence_deps(sum(T_pre[4:7], [])))
                    k.op("act", _act(uprev[:, c, :], u[:, TP:TP + 2], AF.Copy), reads=ts(3 * ub, 3), writes=[T_uprev[c]])
                    if has_s:
                        k.op("act", _act(csall[:, c, 0:32].rearrange("p (b r) -> p b r", b=NSEQ), us[:, :, 8:10], AF.Copy), reads=ts(14, 1), writes=[T_cs[c]])
                    if bi == len(BLOCKS) - 1:
                        k.op("act", _act(csall[:, c, 32:34], u[:, TP:TP + 2], AF.Copy), reads=ts(3 * ub, 3), writes=[T_cs[c]])

                while not adone[0]:
                    att_step()
                _chk("a3")
                for zh in range(2):
                    slot = wload(wa_d[44 + zh * 4:48 + zh * 4].rearrange("b p e -> p b e"), 8192, 4)
                    for bq in range(4):
                        c = zh * 4 + bq
                        for nti in range(nnt):
                            b, o, n = proj_group(slot, bq, nti)
                            st_ = 6 + (nti % 2)
                            sz = scr_f32(st_, 1)[:, 0:n]
                            k.op("act", _act(sz, bank(b)[:, 0:n], AF.Silu), reads=[T_bank[b]], writes=ts(st_, 1))
                            tl = list(range(o // 128, (o + n) // 128))
                            k.op("dve", _tt(yaT[:, c, o:o + n], sz, yaT[:, c, o:o + n], ALU.mult),
                                 reads=ts(st_, 1), writes=[T_ya[c][t_] for t_ in tl])

                _chk("a4")
                for e in range(16):
                    slot = wload(wb_d[e], 6144)
                    wv = wring[:, slot, 0:6144].rearrange("p (k j) -> p k j", j=128)
                    for nti in range(nnt):
                        o, n = ntiles_n[nti]
                        tl = list(range(o // 128, (o + n) // 128))
                        bgc, bga, bac, baa = k.nb(), k.nb(), k.nb(), k.nb()
                        k.group([_mm(bank(bgc)[:, 0:n], wv[:, kc, :], xT[:, kc, o:o + n], kc == 0, kc == 15) for kc in range(16)],
                                reads=[T_w[slot], T_xT[nti]], writes=[T_bank[bgc]])
                        k.group([_mm(bank(bga)[:, 0:n], wv[:, 16 + kc, :], xT[:, kc, o:o + n], kc == 0, kc == 15) for kc in range(16)],
                                reads=[T_w[slot], T_xT[nti]], writes=[T_bank[bga]])
                        k.group([_mm(bank(bac)[:, 0:n], wv[:, 32 + kc, :], ycT[:, kc, o:o + n], kc == 0, kc == 7) for kc in range(8)],
                                reads=[T_w[slot]] + [T_yc[c][nti] for c in range(8)], writes=[T_bank[bac]])
                        k.group([_mm(bank(baa)[:, 0:n], wv[:, 40 + kc, :], yaT[:, kc, o:o + n], kc == 0, kc == 7) for kc in range(8)],
                                reads=[T_w[slot]] + [T_ya[c][t_] for c in range(8) for t_ in tl], writes=[T_bank[baa]])
                        st_ = 4 * (nti % 2)
                        sgc, sga, t1, t2 = (scr_f32(st_ + i, 1)[:, 0:n] for i in range(4))
                        k.op("act", _act(sgc, bank(bgc)[:, 0:n], AF.Sigmoid), reads=[T_bank[bgc]], writes=ts(st_, 1))
                        k.op("act", _act(sga, bank(bga)[:, 0:n], AF.Sigmoid), reads=[T_bank[bga]], writes=ts(st_ + 1, 1))
                        k.op("dve", _tt(t1, sgc, bank(bac)[:, 0:n], ALU.mult), reads=ts(st_, 1) + [T_bank[bac]], writes=ts(st_ + 2, 1))
                        k.op("dve", _tt(t2, sga, bank(baa)[:, 0:n], ALU.mult), reads=ts(st_ + 1, 1) + [T_bank[baa]], writes=ts(st_ + 3, 1))
                        wr = [T_m[e][nti]] + (G2 if (e == 0 and nti == 0) else [])
                        k.op("dve", _tt(mT[:, e, o:o + n], t1, t2, ALU.add), reads=ts(st_ + 2, 2), writes=wr)

                _chk("b")
                stt_ = scr_f32(4, 1)
                T_lnst = [[Trk() for _ in range(4)] for _ in range(9)]
                T_lnmv = [Trk() for _ in range(9)]
                pend_bn = [None]

                def ln_stats(ti, cc, stt_=stt_, T_lnst=T_lnst):
                    st_ = stt_[:, ti * 48:ti * 48 + 24].rearrange("p (c s) -> p c s", c=4)
                    k.op("dve", lambda e_, o_=st_[:, cc, :], i_=pre[:, ti, cc * 512:(cc + 1) * 512]: e_.bn_stats(out=o_, in_=i_),
                         reads=[T_pre[ti][cc]] + ts(4, 1), writes=[T_lnst[ti][cc]])

                def ln_a(ti, stt_=stt_, T_lnst=T_lnst, T_lnmv=T_lnmv, ntl=ntl):
                    if ti < 0 or ti >= ntl:
                        return
                    st_ = stt_[:, ti * 48:ti * 48 + 24].rearrange("p (c s) -> p c s", c=4)
                    mv = stt_[:, ti * 48 + 32:ti * 48 + 34]
                    rs = stt_[:, ti * 48 + 40:ti * 48 + 42]
                    k.op("dve", lambda e_, o_=mv, i_=st_: e_.bn_aggr(out=o_, in_=i_), reads=T_lnst[ti] + ts(4, 1), writes=[T_lnmv[ti]])
                    k.op("act", _act(rs[:, 0:1], mv[:, 1:2], AF.Ln, bias=LN_EPS), reads=[T_lnmv[ti]] + ts(4, 1), writes=[T_lnmv[ti]])
                    k.op("act", _act(rs[:, 0:1], rs[:, 0:1], AF.Exp, scale=-0.5), reads=ts(4, 1), writes=[T_lnmv[ti]])
                    k.op("act", _act(rs[:, 1:2], mv[:, 0:1], AF.Identity, scale=rs[:, 0:1]), reads=ts(4, 1), writes=[T_lnmv[ti]])
                    k.op("act", lambda e_, o_=rs[:, 1:2]: e_.mul(o_, o_, -1.0), reads=ts(4, 1), writes=[T_lnmv[ti]])
                    k.op("act", _act(pre[:, ti, :], pre[:, ti, :], AF.Identity, scale=rs[:, 0:1], bias=rs[:, 1:2]),
                         reads=[T_lnmv[ti]], writes=T_pre[ti])

                def ln_b(ti, rows=rows, ntl=ntl):
                    if ti < 0 or ti >= ntl:
                        return
                    k.op("dve", _tt(pre[:, ti, :], pre[:, ti, :], gb[:, 0, :], ALU.mult), reads=[T_mg], writes=T_pre[ti])
                    eng_b = "pool" if (ti % 2 == 1 and ti < ntl - 2) else "dve"
                    k.op(eng_b, _tt(pre[:, ti, :], pre[:, ti, :], gb[:, 1, :], ALU.add), reads=[T_mg], writes=T_pre[ti])
                    k.dma("sp", y_d[rows[ti]:rows[ti] + 128, :], pre[:, ti, :], reads=T_pre[ti], final=True)

                k.dma("sp", mg[:, 0, :], g_d[:, :], writes=[T_mg])
                k.dma("sp", mg[:, 1, :], b_d[:, :], writes=[T_mg])
                xl_i = [0]
                ln_last = [None]

                def ln_ready(ti):
                    lnq.append((ti, lambda t=ti, f=ln_a: f(t)))
                    if ln_last[0] is not None:
                        lnq.append((ln_last[0], lambda t=ln_last[0], f=ln_b: f(t)))
                    ln_last[0] = ti

                LEAD = 4
                its = [(0, t) for t in range(ntl)] + [(1, t) for t in range(ntl)]
                for t in range(ntl + LEAD):
                    if t < ntl:
                        its.append((2, t))
                    if t - LEAD >= 0:
                        its.append((3, t - LEAD))
                XS = [0, 1, 2, 3, 5, 6, 7, 8]
                LA = 7

                def x32_load(i):
                    if i < len(its):
                        nb2, t2 = its[i]
                        k.dma("sp", scr_f32(XS[i % 8], 1), x_d[rows[t2]:rows[t2] + 128, nb2 * 512:(nb2 + 1) * 512], writes=ts(XS[i % 8], 1))

                for i in range(LA):
                    x32_load(i)
                wslot = {}
                for i, (nb_, ti) in enumerate(its):
                    if nb_ not in wslot:
                        wslot[nb_] = wload(wc_d[nb_], 8192)
                        if nb_ == 2:
                            wslot[3] = wload(wc_d[3], 8192)
                    slot = wslot[nb_]
                    wv = wring[:, slot, :].rearrange("p (k j) -> p k j", j=512)
                    x32_load(i + LA)
                    xs = XS[i % 8]
                    xl_i[0] += 1
                    x32 = scr_f32(xs, 1)
                    b = k.nb()
                    k.group([_mm(bank(b)[:, :], mT[:, kc, ti * 128:(ti + 1) * 128], wv[:, kc, :], kc == 0, kc == 15) for kc in range(16)],
                            reads=[T_w[slot]] + [T_m[e][nt_of(ti)] for e in range(16)], writes=[T_bank[b]])
                    wr = [T_pre[ti][nb_]] + (G1 if xl_i[0] == 1 else [])
                    k.op("dve", _stt(pre[:, ti, nb_ * 512:(nb_ + 1) * 512], x32, ALPHA, bank(b)[:, :], ALU.mult, ALU.add),
                         reads=ts(xs, 1) + [T_bank[b]], writes=wr)
                    if pend_bn[0] is not None:
                        ln_stats(*pend_bn[0])
                        if pend_bn[0][1] == 3:
                            ln_ready(pend_bn[0][0])
                    pend_bn[0] = (ti, nb_)
                    if lnq:
                        lnq.pop(0)[1]()
                ln_stats(*pend_bn[0])
                ln_ready(pend_bn[0][0])
                pend_bn[0] = None
                lnq.append((ln_last[0], lambda t=ln_last[0], f=ln_b: f(t)))
                if bi == len(BLOCKS) - 1:
                    while lnq:
                        lnq.pop(0)[1]()

            b = k.nb()
            fns = [_tr(bank(b)[0:34, c * 128:(c + 1) * 128], csall[:, c, :], identf[:]) for c in range(4)]
            k.group(fns, reads=T_cs + [T_const], writes=[T_bank[b]])
            b2 = k.nb()
            fns = [_tr(bank(b2)[0:34, (c - 4) * 128:(c - 3) * 128], csall[:, c, :], identf[:]) for c in range(4, 8)]
            k.group(fns, reads=T_cs + [T_const], writes=[T_bank[b2]])
            cso = scr_f32(8, 2)[0:34, :]
            k.op("dve", _cp(cso[:, 0:512], bank(b)[0:34, :]), reads=[T_bank[b]], writes=ts(8, 2))
            k.op("dve", _cp(cso[:, 512:1024], bank(b2)[0:34, :]), reads=[T_bank[b2]], writes=ts(8, 2))
            k.dma("sp", cst_d[:, :], cso, reads=ts(8, 2), final=True)
            k.st["sp"].add(lambda e: None, k.final, None)

        except _Stop:
            k.st["sp"].add(lambda e: None, k.final, None)
        with nc.Block() as block:
            block.tensor(lambda e: k.st["pe"].emit(e))
            block.scalar(lambda e: k.st["act"].emit(e))
            block.vector(lambda e: k.st["dve"].emit(e))
            block.gpsimd(lambda e: k.st["pool"].emit(e))
            block.sync(lambda e: k.st["sp"].emit(e))
    return nc


def _host_tables():
    h = np.arange(1, NH + 1, dtype=np.float64)
    slopes = 2.0 ** (-8.0 * h / NH)
    j = np.arange(128)[:, None, None].astype(np.float64)
    i = np.arange(128)[None, None, :].astype(np.float64)
    sl = slopes[None, :, None]
    mprev = np.where(j >= i, np.exp(-sl * (128 + i - j)), 0.0)
    mown = np.where(j <= i, np.exp(-sl * (i - j)), 0.0)
    bj, pj = np.divmod(np.arange(128), 8)
    same = (bj[:, None] == bj[None, :]) & (pj[:, None] <= pj[None, :])
    dl = (pj[None, :] - pj[:, None]).astype(np.float64)
    msnew = np.where(same[:, None, :], np.exp(-sl * dl[:, None, :]), 0.0)
    p = np.arange(8)[None, None, :].astype(np.float64)
    jj = np.arange(128)[:, None, None].astype(np.float64)
    msc_h = np.where(jj >= p, np.exp(-sl * (128 + p - jj)), 0.0)
    order = [4 * g + 2 * kk + hp for hp in range(2) for g in range(4) for kk in range(2)]
    msc = msc_h[:, order, :]
    f = lambda a: np.ascontiguousarray(a.reshape(128, -1)).astype(np.float32)
    return f(mprev), f(mown), f(msnew), f(msc)


_NC_CACHE = {}


def kernel(x_prompt, x_sample, cache_k, cache_v, state_conv, w_in, conv_w, attn_sinks,
           w_conv_out, w_attn_out, w_out, ln_g, ln_b):
    f32 = np.float32
    w_in = np.asarray(w_in, f32)[0]
    wco = np.asarray(w_conv_out, f32)[0]
    wao = np.asarray(w_attn_out, f32)[0]
    wo = np.asarray(w_out, f32)[0]

    def blk(w, c0, nk):
        return w[:, c0:c0 + 128].reshape(nk, 128, 128).transpose(1, 0, 2)

    cols = [4096 + 128 * c for c in range(8)] + [5120, 5248, 5376, 5504]
    for c in range(8):
        cols += [1024 + 128 * c, 2048 + 128 * c, 0 + 128 * c, 3072 + 128 * c]
    cols += [5632 + 128 * c for c in range(8)]
    wa = np.stack([blk(w_in, c0, 16) for c0 in cols]).reshape(52, 128, 16 * 128)
    wb = np.stack([np.concatenate([blk(w_in, 6656 + 128 * e, 16), blk(w_in, 8704 + 128 * e, 16),
                                   blk(wco, 128 * e, 8), blk(wao, 128 * e, 8)], axis=1) for e in range(16)]).reshape(16, 128, 48 * 128)
    wc = np.stack([wo[:, n * 512:(n + 1) * 512].reshape(16, 128, 512).transpose(1, 0, 2) for n in range(4)]).reshape(4, 128, 16 * 512)
    cw = np.ascontiguousarray(np.asarray(conv_w, f32)[0].reshape(3, 8, 128).transpose(2, 1, 0)).reshape(128, 24)
    sinks = np.ascontiguousarray(np.broadcast_to(np.asarray(attn_sinks, f32)[0][None, :], (128, 16)))
    lng = np.ascontiguousarray(np.broadcast_to(np.asarray(ln_g, f32)[0][None, :], (128, D)))
    lnb = np.ascontiguousarray(np.broadcast_to(np.asarray(ln_b, f32)[0][None, :], (128, D)))
    mprev, mown, msnew, msc = _host_tables()
    shared = {"wa": np.ascontiguousarray(wa), "wb": np.ascontiguousarray(wb), "wc": np.ascontiguousarray(wc),
              "cw": cw, "sinks": sinks, "lng": lng, "lnb": lnb, "ident": np.eye(128, dtype=f32),
              "mprev": mprev, "mown": mown, "msnew": msnew, "mscache": msc}
    xp = np.asarray(x_prompt, f32)
    xs = np.asarray(x_sample, f32)
    ck = np.asarray(cache_k, f32)[0].reshape(128, WIN, 256)
    cv = np.asarray(cache_v, f32)[0].reshape(128, WIN, 256)
    sc = np.asarray(state_conv, f32)[0]
    in_maps = []
    for c in range(NCORES):
        m = dict(shared)
        m["x"] = np.ascontiguousarray(np.concatenate([xp[c], xs[NSEQ * c:NSEQ * (c + 1)].reshape(NSEQ * DS, D)], axis=0))
        m["ck"] = np.ascontiguousarray(ck[NSEQ * c:NSEQ * (c + 1)])
        m["cv"] = np.ascontiguousarray(cv[NSEQ * c:NSEQ * (c + 1)])
        m["sc"] = np.ascontiguousarray(sc[NSEQ * c:NSEQ * (c + 1)].reshape(NSEQ * 2, DC))
        in_maps.append(m)
    if "nc" not in _NC_CACHE:
        _NC_CACHE["nc"] = build_nc()
    res = run_bass_kernel_spmd(_NC_CACHE["nc"], in_maps, core_ids=list(range(NCORES)))
    R = res.results
    y_prompt = np.stack([R[c]["y"][0:SEQ] for c in range(NCORES)]).astype(f32)
    y_sample = np.concatenate([R[c]["y"][SEQ:].reshape(NSEQ, DS, D) for c in range(NCORES)], axis=0).astype(f32)
    k_prompt = np.stack([R[c]["kvp"][:, 0:256].reshape(WIN, NKV, HD) for c in range(NCORES)])[None].astype(f32)
    v_prompt = np.stack([R[c]["kvp"][:, 256:512].reshape(WIN, NKV, HD) for c in range(NCORES)])[None].astype(f32)
    conv_prompt = np.stack([R[c]["cst"][32:34] for c in range(NCORES)])[None].astype(f32)
    k_sample = np.concatenate([R[c]["ks"].reshape(NSEQ, WIN, NKV, HD) for c in range(NCORES)], axis=0)[None].astype(f32)
    v_sample = np.concatenate([R[c]["vs"].reshape(NSEQ, WIN, NKV, HD) for c in range(NCORES)], axis=0)[None].astype(f32)
    conv_sample = np.concatenate([R[c]["cst"][0:32].reshape(NSEQ, 2, DC) for c in range(NCORES)], axis=0)[None].astype(f32)
    return (y_prompt, y_sample, k_prompt, v_prompt, conv_prompt, k_sample, v_sample, conv_sample)
```

```python
import os
import numpy as np
from contextlib import ExitStack
import concourse.bass as bass
import concourse.mybir as mybir
from concourse.bass_utils import run_bass_kernel_spmd

F32, BF16 = mybir.dt.float32, mybir.dt.bfloat16
AF = mybir.ActivationFunctionType
ALU = mybir.AluOpType

D = 2048; DC = 1024; NH = 16; NKV = 4; HD = 64; WIN = 128
NCORES = 8; SEQ = 2048; NSEQ = 16; DS = 8
NTOK = SEQ + NSEQ * DS
TMAX = 1152
ALPHA = float(2.0 ** 0.25)
LN_EPS = 1e-5
BLOCKS = [(list(range(0, 8)), True), (list(range(8, 16)), False)]
NDMASEM = 40


class _Stop(Exception):
    pass


def _chk(name):
    if os.environ.get("KSTOP", "") == name:
        raise _Stop()


class Trk:
    __slots__ = ("w", "r", "excl", "bank_id")

    def __init__(self, excl=False, bank_id=None):
        self.w = None
        self.r = {}
        self.bank_id = bank_id
        self.excl = excl


class Stream:
    def __init__(self, name, sem):
        self.name, self.sem, self.n, self.ops, self.seen = name, sem, 0, [], {}

    def add(self, fn, deps, inc):
        waits = []
        for d in deps:
            if d is None:
                continue
            sem, val, key = d
            if key == "pe" and self.name == "pe":
                continue
            if self.seen.get(key, 0) >= val:
                continue
            self.seen[key] = val
            waits.append((sem, val))
        self.ops.append((waits, fn, inc))

    def emit(self, eng):
        for waits, fn, inc in self.ops:
            for sem, val in waits:
                eng.wait_ge(sem, val)
            ins = fn(eng)
            if inc is not None:
                ins.then_inc(inc[0], inc[1])


class K:
    def __init__(self, nc, es):
        self.nc, self.es = nc, es
        self.st = {}
        for nm in ("pe", "act", "dve", "pool", "sp"):
            self.st[nm] = Stream(nm, es.enter_context(nc.semaphore("s_" + nm)))
        self.dsems = {"sp": [es.enter_context(nc.semaphore("d%d" % i)) for i in range(NDMASEM)],
                      "pool": [es.enter_context(nc.semaphore("g%d" % i)) for i in range(16)],
                      "act": [es.enter_context(nc.semaphore("a%d" % i)) for i in range(6)]}
        self.ndma = {"sp": 0, "pool": 0, "act": 0}
        self.final = []
        self.bank_rr = 0
        self.reserved = set()
        self.held = set()

    def deps(self, sname, reads, writes, extra=()):
        deps = list(extra)
        for t in reads:
            deps.append(t.w)
            if t.excl:
                for key, tok in t.r.items():
                    if key != sname:
                        deps.append(tok)
        for t in writes:
            deps.append(t.w)
            for key, tok in t.r.items():
                deps.append(tok)
        return deps

    def mark(self, tok, rkey, reads, writes):
        for t in reads:
            t.r[rkey] = tok
        for t in writes:
            t.w = tok
            t.r = {}

    def op(self, sname, fn, reads=(), writes=(), extra=(), hold=()):
        for t in reads:
            if t.bank_id is not None and t.bank_id not in hold:
                self.held.discard(t.bank_id)
        s = self.st[sname]
        deps = self.deps(sname, reads, writes, extra)
        s.n += 1
        tok = (s.sem, s.n, sname)
        s.add(fn, deps, (s.sem, 1))
        self.mark(tok, sname, reads, writes)
        return tok

    def group(self, fns, reads=(), writes=(), extra=()):
        s = self.st["pe"]
        deps = self.deps("pe", reads, writes, extra)
        s.n += 1
        tok = (s.sem, s.n, "pe")
        for i, fn in enumerate(fns):
            s.add(fn, deps if i == 0 else (), (s.sem, 1) if i == len(fns) - 1 else None)
        self.mark(tok, "pe", reads, writes)
        return tok

    def dma(self, sname, out, in_, reads=(), writes=(), extra=(), final=False):
        s = self.st[sname]
        i = self.ndma[sname]
        self.ndma[sname] += 1
        pool_ = self.dsems[sname]
        sem = pool_[i % len(pool_)]
        val = 16 * (i // len(pool_) + 1)
        key = "dma_%s%d" % (sname, i % len(pool_))
        deps = self.deps(key, reads, writes, extra)
        if val > 16:
            deps.append((sem, val - 16, key))
        s.add(lambda e, o=out, n=in_: e.dma_start(out=o, in_=n), deps, (sem, 16))
        tok = (sem, val, key)
        self.mark(tok, key, reads, writes)
        if final:
            self.final.append(tok)
        return tok

    def nb(self):
        for _ in range(16):
            b = self.bank_rr % 8
            self.bank_rr += 1
            if b not in self.reserved and b not in self.held:
                self.held.add(b)
                return b
        raise RuntimeError("out of PSUM banks: held=%s reserved=%s" % (self.held, self.reserved))


def _mm(out, lhsT, rhs, start, stop, tp=None):
    if tp is None:
        return lambda e: e.matmul(out, lhsT=lhsT, rhs=rhs, start=start, stop=stop)
    return lambda e: e.matmul(out, lhsT=lhsT, rhs=rhs, start=start, stop=stop, tile_position=tp)


def _tr(out, in_, ident):
    return lambda e: e.transpose(out, in_, ident)


def _act(out, in_, func, scale=1.0, bias=0.0):
    return lambda e: e.activation(out=out, in_=in_, func=func, bias=bias, scale=scale)


def _tt(out, in0, in1, op):
    return lambda e: e.tensor_tensor(out=out, in0=in0, in1=in1, op=op)


def _stt(out, in0, scalar, in1, op0, op1):
    return lambda e: e.scalar_tensor_tensor(out=out, in0=in0, scalar=scalar, in1=in1, op0=op0, op1=op1)


def _ts(out, in0, s1, op0):
    return lambda e: e.tensor_scalar(out=out, in0=in0, scalar1=s1, scalar2=None, op0=op0)


def _cp(out, in_):
    return lambda e: e.tensor_copy(out=out, in_=in_)


def _ms(ap, v):
    return lambda e: e.memset(ap, v)


def build_nc():
    nc = bass.Bass("TRN2", target_bir_lowering=False)

    def din(name, shape):
        return nc.dram_tensor(name, list(shape), F32, kind="ExternalInput").ap()

    def dout(name, shape):
        return nc.dram_tensor(name, list(shape), F32, kind="ExternalOutput").ap()

    x_d = din("x", [NTOK, D])
    ck_d = din("ck", [NSEQ, WIN, 256])
    cv_d = din("cv", [NSEQ, WIN, 256])
    sc_d = din("sc", [NSEQ * 2, DC])
    wa_d = din("wa", [52, 128, 16 * 128])
    wb_d = din("wb", [16, 128, 48 * 128])
    wc_d = din("wc", [4, 128, 16 * 512])
    cw_d = din("cw", [128, 24])
    sink_d = din("sinks", [128, 16])
    g_d = din("lng", [128, D])
    b_d = din("lnb", [128, D])
    id_d = din("ident", [128, 128])
    mp_d = din("mprev", [128, NH * 128])
    mo_d = din("mown", [128, NH * 128])
    msn_d = din("msnew", [128, NH * 128])
    msc_d = din("mscache", [128, NH * 8])

    y_d = dout("y", [NTOK, D])
    kvp_d = dout("kvp", [128, 512])
    cst_d = dout("cst", [34, DC])
    ks_d = dout("ks", [NSEQ, WIN, 256])
    vs_d = dout("vs", [NSEQ, WIN, 256])

    with ExitStack() as es:
        k = K(nc, es)

        def sb(name, shape, dt):
            return es.enter_context(nc.sbuf_tensor(name, list(shape), dt))

        big1 = sb("big1", [128, 36864], BF16)
        big2 = sb("big2", [128, 18432], BF16)
        wring = sb("wring", [128, 2, 8192], BF16)
        mg = sb("mg", [128, 2, D], F32)
        mtab = mg[:, :, :].rearrange("p k (h q) -> p k h q", h=NH)
        gb = mg
        T_mg = Trk()
        identf = sb("identf", [128, 128], F32)
        identb = sb("identb", [128, 128], BF16)
        esink = sb("esink", [128, 16], F32)
        cwt = sb("cwt", [128, 8, 3], F32)
        kprevA = sb("kprevA", [128, 2, 128], BF16)
        kprevB = sb("kprevB", [128, 2, 128], BF16)
        vprev = sb("vprev", [128, 4, 65], BF16)
        uprev = sb("uprev", [128, 8, 2], F32)
        csall = sb("csall", [128, 8, 34], F32)
        scT = sb("scT", [128, 8, 32], F32)
        NSLOT = 23
        scr = sb("scr", [128, NSLOT, 1024], BF16)
        ps = es.enter_context(nc.psum_tensor("ps", [128, 8, 512], F32))

        xT = big1[:, 0:16 * TMAX].rearrange("p (k t) -> p k t", k=16)
        ycT = big1[:, 16 * TMAX:24 * TMAX].rearrange("p (k t) -> p k t", k=8)
        yaT = big1[:, 24 * TMAX:32 * TMAX].rearrange("p (k t) -> p k t", k=8)
        pre = big1[:, :].bitcast(F32).rearrange("p (t d) -> p t d", d=D)
        mT = big2[:, 0:16 * TMAX].rearrange("p (k t) -> p k t", k=16)
        qT = big2[:, 0:8 * TMAX].rearrange("p (k t) -> p k t", k=8)
        kTA = big2[:, 8 * TMAX:10 * TMAX].rearrange("p (k t) -> p k t", k=2)
        kTB = big2[:, 10 * TMAX:12 * TMAX].rearrange("p (k t) -> p k t", k=2)
        vaug = big2[:, 12 * TMAX:12 * TMAX + 9 * 260].rearrange("p (t g d) -> p t g d", t=9, g=4)

        ycs = big1[:, 16 * TMAX:24 * TMAX]
        msnew_v = ycs[:, 0:4096].bitcast(F32)
        msc_v = ycs[:, 4096:4352].bitcast(F32)
        Ppad_flat = ycs[:, 4352:6400]
        cvaug_flat = ycs[:, 6400:7440]
        T_ys_m, T_ys_p, T_ys_v = Trk(), Trk(), Trk()
        T_ys_extra = []

        def sl_f32(s0, n):
            return scr[:, s0, :].bitcast(F32) if n == 512 else None

        def scr_f32(s0, nslots):
            return scr[:, s0:s0 + nslots, :].rearrange("p s e -> p (s e)").bitcast(F32)

        def scr_bf(s0, nslots):
            return scr[:, s0:s0 + nslots, :].rearrange("p s e -> p (s e)")

        T_scr = [Trk() for _ in range(NSLOT)]

        def ts(s0, n=1):
            return T_scr[s0:s0 + n]

        T_bank = [Trk(excl=True, bank_id=i) for i in range(8)]
        T_w = [Trk(), Trk()]
        T_wb = [[Trk() for _ in range(4)] for _ in range(2)]
        T_const = Trk()
        T_xT = [Trk() for _ in range(3)]
        T_q = [[Trk() for _ in range(3)] for _ in range(8)]
        T_kT = [[Trk() for _ in range(3)] for _ in range(2)]
        T_v = [Trk() for _ in range(9)]
        T_yc = [[Trk() for _ in range(3)] for _ in range(8)]
        T_ya = [[Trk() for _ in range(9)] for _ in range(8)]
        T_m = [[Trk() for _ in range(3)] for _ in range(16)]
        T_pre = [[Trk() for _ in range(4)] for _ in range(9)]
        T_kprev = Trk()
        T_uprev = [Trk() for _ in range(8)]
        T_cs = [Trk() for _ in range(8)]
        G1 = T_xT + sum(T_yc, []) + sum(T_ya, []) + sum(T_pre, [])
        G2 = sum(T_q, []) + sum(T_kT, []) + T_v + sum(T_m, [])

        def bank(b):
            return ps[:, b, :]

        def bankb(b):
            return ps[:, b, :].bitcast(BF16)

        cl = []
        cl.append(k.dma("sp", identf[:], id_d[:, :], writes=[T_const]))
        cl.append(k.dma("pool", identb[:], id_d[:, :], writes=[Trk()]))
        cl.append(k.dma("sp", esink[:], sink_d[:, :], writes=[Trk()]))
        cl.append(k.dma("sp", cwt[:], cw_d.rearrange("p (c t) -> p c t", c=8), writes=[Trk()]))
        scrows = scr_f32(0, 2)[0:32, :]
        cl.append(k.dma("sp", scrows, sc_d[:, :], writes=ts(0, 2)))
        CT = Trk()
        CT.w = None
        consts_ready = cl
        t_es = k.op("act", _act(esink[:], esink[:], AF.Exp), extra=consts_ready, writes=[T_const])
        b0 = k.nb()
        fns = [_tr(bank(b0)[:, c * 32:(c + 1) * 32], scrows[:, c * 128:(c + 1) * 128], identf[0:32, 0:32]) for c in range(8)]
        k.group(fns, reads=ts(0, 2) + [T_const], writes=[T_bank[b0]], extra=consts_ready)
        k.op("dve", _cp(scT[:], bank(b0)[:, 0:256].rearrange("p (c r) -> p c r", c=8)),
             reads=[T_bank[b0]], writes=[T_const])
        k.op("dve", _ms(uprev[:], 0.0), writes=T_uprev)

        lnq = []

        def fence_deps(trks):
            d = []
            for t in trks:
                d.append(t.w)
                d.extend(t.r.values())
            return d

        wdma_i = [0]

        def wload(src_ap, nelem, nb4=0, extra=()):
            s = wdma_i[0] % 2
            wdma_i[0] += 1
            dst = wring[:, s, 0:nelem]
            if nb4:
                dst = dst.rearrange("p (b e) -> p b e", b=nb4)
            k.dma("pool", dst, src_ap, writes=[T_w[s]] + T_wb[s], extra=extra)
            return s

        def wload_split(blk0):
            s = wdma_i[0] % 2
            wdma_i[0] += 1
            for j in range(4):
                k.dma("pool", wring[:, s, j * 2048:(j + 1) * 2048], wa_d[blk0 + j], writes=[T_wb[s][j]] + ([T_w[s]] if j == 0 else []))
            return s

        row0 = {True: 0}

        try:
            for bi, (ptiles, has_s) in enumerate(BLOCKS if not os.environ.get("KSTOP") else BLOCKS[:1]):
                ntl = len(ptiles) + (1 if has_s else 0)
                T = ntl * 128
                TP = len(ptiles) * 128
                rows = [pt * 128 for pt in ptiles] + ([SEQ] if has_s else [])
                ntiles_n = [(o, min(512, T - o)) for o in range(0, T, 512)]
                nnt = len(ntiles_n)

                def nt_of(ti):
                    return ti // 4

                while any(t_ <= 4 for t_, _f in lnq):
                    lnq.pop(0)[1]()
                pre_slots = []
                XB4 = [0, 2, 6, 8]
                xtoks = []
                if bi == 0:
                    pre_slots.append(wload_split(0))
                else:
                    for tj in range(4):
                        sj = XB4[tj % 4]
                        k.dma("pool", scr_bf(sj, 2), x_d[rows[tj]:rows[tj] + 128, :], writes=ts(sj, 2))
                    pre_slots.append(wload_split(0))
                for ti in range(ntl):
                    if bi == 0:
                        s = ti % 2
                        xsl = 2 * s
                        xb = scr_bf(xsl, 2)
                        stg = scr_f32(6 + 4 * s, 4)
                        xtoks.append(k.dma("sp", stg, x_d[rows[ti]:rows[ti] + 128, :], writes=ts(6 + 4 * s, 4)))
                        if ti == ntl - 3:
                            pre_slots.append(wload(wa_d[4:8].rearrange("b p e -> p b e"), 8192, 4, extra=xtoks[-1:]))
                        if ti % 2 == 0:
                            k.op("dve", _cp(xb, stg), reads=ts(6 + 4 * s, 4), writes=ts(xsl, 2))
                        else:
                            k.op("act", _act(xb, stg, AF.Copy), reads=ts(6 + 4 * s, 4), writes=ts(xsl, 2))
                    else:
                        xsl = XB4[ti % 4]
                        xb = scr_bf(xsl, 2)
                    for hf in range(2):
                        b = k.nb()
                        fns = [_tr(bankb(b)[:, j * 128:(j + 1) * 128], xb[:, (hf * 8 + j) * 128:(hf * 8 + j + 1) * 128], identb[:])
                               for j in range(8)]
                        k.group(fns, reads=ts(xsl, 2) + [T_const], writes=[T_bank[b]], extra=consts_ready)
                        eng = "act" if (hf == 0 or bi > 0) else "dve"
                        src = bankb(b).rearrange("p (j t) -> p j t", j=8)
                        dst = xT[:, hf * 8:(hf + 1) * 8, ti * 128:(ti + 1) * 128]
                        first = (ti == 0 and hf == 0)
                        wr = [T_xT[nt_of(ti)]] + (T_xT if first else [])
                        ex = fence_deps(sum(T_pre[0:5], [])) if first else []
                        if eng == "act":
                            k.op("act", _act(dst, src, AF.Copy), reads=[T_bank[b]], writes=wr, extra=ex)
                        else:
                            k.op("dve", _cp(dst, src), reads=[T_bank[b]], writes=wr, extra=ex)
                        if lnq and (2 * ti + hf) % 3 == 2:
                            lnq.pop(0)[1]()
                    if bi > 0:
                        tj = ti + 4
                        if tj < ntl:
                            sj = XB4[tj % 4]
                            k.dma("pool", scr_bf(sj, 2), x_d[rows[tj]:rows[tj] + 128, :], writes=ts(sj, 2))
                        if tj == ntl - 1 or (ntl <= 4 and ti == ntl - 1):
                            pre_slots.append(wload(wa_d[4:8].rearrange("b p e -> p b e"), 8192, 4))

                _chk("s0")
                def proj_group(slot, blk, nti, nkc=16, rhsT=None, kbase=0):
                    o, n = ntiles_n[nti]
                    wv = wring[:, slot, :].rearrange("p (b k j) -> p b k j", k=16, j=128)
                    b = k.nb()
                    fns = [_mm(bank(b)[:, 0:n], wv[:, blk, kc, :], xT[:, kc, o:o + n], kc == 0, kc == 15) for kc in range(16)]
                    k.group(fns, reads=[T_w[slot], T_wb[slot][blk], T_xT[nti]], writes=[T_bank[b]])
                    return b, o, n

                evac_rr = [0]

                def evac_copy(dst, src, reads, writes):
                    if lnq:
                        lnq.pop(0)[1]()
                    evac_rr[0] += 1
                    if evac_rr[0] % 2 == 0:
                        return k.op("act", _act(dst, src, AF.Copy), reads=reads, writes=writes)
                    return k.op("dve", _cp(dst, src), reads=reads, writes=writes)

                for qh in range(2):
                    slot = pre_slots[qh]
                    for bq in range(4):
                        c = qh * 4 + bq
                        for nti in range(nnt):
                            b, o, n = proj_group(slot, bq, nti)
                            wr = [T_q[c][nti]] + (G2 if (qh == 0 and bq == 0 and nti == 0) else [])
                            evac_copy(qT[:, c, o:o + n], bank(b)[:, 0:n], [T_bank[b]], wr)
                while lnq:
                    lnq.pop(0)[1]()
                k.dma("sp", mg[:, 0, :], mp_d[:, :], writes=[T_mg])
                k.dma("sp", mg[:, 1, :], mo_d[:, :], writes=[T_mg])
                if bi == 0:
                    k.dma("sp", msnew_v, msn_d[:, :], writes=[T_ys_m] + sum(T_yc, []))
                    k.dma("sp", msc_v, msc_d[:, :], writes=[T_ys_m])
                _chk("a1q")
                slot = wload(wa_d[8:12].rearrange("b p e -> p b e"), 8192, 4)
                for j in range(2):
                    for nti in range(nnt):
                        b, o, n = proj_group(slot, j, nti)
                        k.op("act", _act(kTA[:, j, o:o + n], bank(b)[:, 0:n], AF.Copy), reads=[T_bank[b]], writes=[T_kT[j][nti]], hold=[b])
                        k.op("dve", _cp(kTB[0:64, j, o:o + n], bank(b)[64:128, 0:n]), reads=[T_bank[b]], writes=[T_kT[j][nti]], hold=[b])
                        k.op("act", _act(kTB[64:128, j, o:o + n], bank(b)[0:64, 0:n], AF.Copy), reads=[T_bank[b]], writes=[T_kT[j][nti]])
                _chk("a1k")
                wv = wring[:, slot, :].rearrange("p (b k j) -> p b k j", k=16, j=128)
                if "vms" not in os.environ.get("KSKIP", ""):
                    k.op("dve", _ms(vaug[:, 0:ntl, :, 64:65], 1.0), writes=T_v[0:ntl])
                for ti in range(ntl):
                    is_s = has_s and ti == ntl - 1
                    kout = is_s or (rows[ti] == SEQ - 128)
                    b = k.nb()
                    blist = [0, 1, 2, 3] if kout else [2, 3]
                    voff = 256 if kout else 0
                    fns = []
                    for jj, wb_ in enumerate(blist):
                        fns += [_mm(bank(b)[:, jj * 128:(jj + 1) * 128], xT[:, kc, ti * 128:(ti + 1) * 128], wv[:, wb_, kc, :], kc == 0, kc == 15)
                                for kc in range(16)]
                    k.group(fns, reads=[T_w[slot], T_xT[nt_of(ti)]], writes=[T_bank[b]])
                    if "vev" not in os.environ.get("KSKIP", ""):
                        k.op("act", _act(vaug[:, ti, :, 0:64], bank(b)[:, voff:voff + 256].rearrange("p (g d) -> p g d", g=4), AF.Copy),
                             reads=[T_bank[b]], writes=[T_v[ti]], hold=[b] if kout else [])
                    if kout and "vkv" not in os.environ.get("KSKIP", ""):
                        kvo = scr_f32(4, 1)
                        k.op("dve", _cp(kvo, bank(b)[:, :]), reads=[T_bank[b]], writes=ts(4, 1))
                        if is_s and "sout" in os.environ.get("KSKIP", ""):
                            pass
                        elif is_s:
                            for sq in range(NSEQ):
                                k.dma("sp", ks_d[sq, WIN - DS:WIN, :], kvo[sq * 8:(sq + 1) * 8, 0:256], reads=ts(4, 1), final=True)
                                k.dma("sp", vs_d[sq, WIN - DS:WIN, :], kvo[sq * 8:(sq + 1) * 8, 256:512], reads=ts(4, 1), final=True)
                        else:
                            k.dma("sp", kvp_d[:, :], kvo, reads=ts(4, 1), final=True)
                if has_s and "d2d" not in os.environ.get("KSKIP", ""):
                    k.dma("sp", ks_d[:, 0:WIN - DS, :], ck_d[:, DS:WIN, :], final=True)
                    k.dma("sp", vs_d[:, 0:WIN - DS, :], cv_d[:, DS:WIN, :], final=True)

                _chk("a1")
                E_sb = [scr_f32(15, 2), scr_f32(17, 2)]
                P_sb = [scr_bf(19, 1), scr_bf(20, 1)]
                o_sb = scr_bf(21, 1)
                den = scr_f32(22, 1)
                pb_rr = [0]
                T_P = [[Trk(), Trk()], [Trk(), Trk()]]

                def att_scores(ti, g, blks):
                    sbanks = []
                    for hp in range(2):
                        b = k.nb()
                        fns = []
                        rd = [T_kT[g // 2][nt_of(ti)]]
                        useA = (g % 2) == hp
                        for bi2, kind in enumerate(blks):
                            if kind == "own":
                                kt = (kTA if useA else kTB)[hp * 64:(hp + 1) * 64, g // 2, ti * 128:(ti + 1) * 128]
                            elif ti > 0:
                                kt = (kTA if useA else kTB)[hp * 64:(hp + 1) * 64, g // 2, (ti - 1) * 128:ti * 128]
                                rd.append(T_kT[g // 2][nt_of(ti - 1)])
                            else:
                                kt = (kprevA if useA else kprevB)[hp * 64:(hp + 1) * 64, g // 2, :]
                                rd.append(T_kprev)
                            for kk in range(2):
                                h = 4 * g + 2 * kk + hp
                                qt = qT[hp * 64:(hp + 1) * 64, h // 2, ti * 128:(ti + 1) * 128]
                                rd.append(T_q[h // 2][nt_of(ti)])
                                c0 = (bi2 * 2 + kk) * 128
                                fns.append(_mm(bank(b)[:, c0:c0 + 128], kt, qt, True, True))
                        k.group(fns, reads=rd, writes=[T_bank[b]])
                        sbanks.append(b)
                    return sbanks

                def att_blks(ti, is_s):
                    gti = rows[ti] // 128
                    return ["own"] if (is_s or gti == 0) else ["prev", "own"]

                def att_S(ti, g, is_s):
                    blks = att_blks(ti, is_s)
                    nbk = len(blks)
                    n_ = nbk * 256
                    sbanks = att_scores(ti, g, blks)
                    pb = pb_rr[0] % 2
                    pb_rr[0] += 1
                    Ev = E_sb[pb].rearrange("p (b e) -> p b e", b=2)
                    Pv = P_sb[pb].rearrange("p (b e) -> p b e", b=2)
                    for hp in range(2):
                        k.op("act", _act(Ev[:, hp, 0:n_], bank(sbanks[hp])[:, 0:n_], AF.Exp, scale=0.125),
                             reads=[T_bank[sbanks[hp]]], writes=ts(15 + 2 * pb + hp, 1))
                        h0 = 4 * g + hp
                        if is_s:
                            mt = msnew_v.rearrange("p (h q) -> p h q", h=NH)[:, h0:h0 + 3:2, :]
                            mrd = [T_ys_m]
                        elif nbk == 2:
                            mt = mtab[:, 0:2, h0:h0 + 3:2, :]
                            mrd = [T_mg]
                        else:
                            mt = mtab[:, 1, h0:h0 + 3:2, :]
                            mrd = [T_mg]
                        if nbk == 2:
                            pv_ = Pv[:, hp, 0:n_].rearrange("p (b h q) -> p b h q", b=2, h=2)
                            ev_ = Ev[:, hp, 0:n_].rearrange("p (b h q) -> p b h q", b=2, h=2)
                        else:
                            pv_ = Pv[:, hp, 0:n_].rearrange("p (h q) -> p h q", h=2)
                            ev_ = Ev[:, hp, 0:n_].rearrange("p (h q) -> p h q", h=2)
                        k.op("dve", _tt(pv_, ev_, mt, ALU.mult), reads=ts(15 + 2 * pb + hp, 1) + mrd, writes=[T_P[pb][hp]], extra=consts_ready)
                    return pb

                def att_PV(ti, g, is_s, pb):
                    blks = att_blks(ti, is_s)
                    nbk = len(blks)
                    Pv = P_sb[pb].rearrange("p (b e) -> p b e", b=2)
                    ob = (4 + g) if is_s else k.nb()
                    fns = []
                    rd = T_P[pb] + [T_v[ti]]
                    for hh in range(4):
                        kk, hp = hh // 2, hh % 2
                        for bi2, kind in enumerate(blks):
                            if kind == "own":
                                vv = vaug[:, ti, g, :]
                            elif ti > 0:
                                vv = vaug[:, ti - 1, g, :]
                                rd.append(T_v[ti - 1])
                            else:
                                vv = vprev[:, g, :]
                                rd.append(T_kprev)
                            c0 = (bi2 * 2 + kk) * 128
                            fns.append(_mm(bank(ob)[:, hh * 65:(hh + 1) * 65], Pv[:, hp, c0:c0 + 128], vv,
                                           (bi2 == 0 and hh == 0) if is_s else (bi2 == 0), (bi2 == nbk - 1) and not is_s))
                    k.group(fns, reads=rd, writes=[T_bank[ob]])
                    if not is_s:
                        att_finish_group(ti, g, ob)
                    return ob

                def att_tile(ti, is_s):
                    for g in range(4):
                        pb = att_S(ti, g, is_s)
                        att_PV(ti, g, is_s, pb)

                def att_gen(ntp):
                    for ti in range(ntp):
                        pbs = [None] * 4
                        pbs[0] = att_S(ti, 0, False)
                        yield
                        pbs[1] = att_S(ti, 1, False)
                        yield
                        att_PV(ti, 0, False, pbs[0])
                        pbs[2] = att_S(ti, 2, False)
                        yield
                        att_PV(ti, 1, False, pbs[1])
                        pbs[3] = att_S(ti, 3, False)
                        yield
                        att_PV(ti, 2, False, pbs[2])
                        yield
                        att_PV(ti, 3, False, pbs[3])
                        yield
                        att_transpose(ti)
                        yield

                def att_finish_group(ti, g, ob):
                    ov = bank(ob)[:, 0:260].rearrange("p (h d) -> p h d", h=4)
                    dn = den[:, g * 8:g * 8 + 4]
                    rdn = den[:, g * 8 + 4:g * 8 + 8]
                    k.op("dve", _tt(dn.rearrange("p (h o) -> p h o", o=1), ov[:, :, 64:65],
                                    esink[:, 4 * g:4 * g + 4].rearrange("p (h o) -> p h o", o=1), ALU.add),
                         reads=[T_bank[ob], T_const], writes=ts(22, 1), hold=[ob])
                    k.op("dve", lambda e, a=rdn, b_=dn: e.reciprocal(out=a, in_=b_), reads=ts(22, 1), writes=ts(22, 1))
                    osv = o_sb.rearrange("p (h d) -> p h d", h=NH)[:, 4 * g:4 * g + 4, :]
                    k.op("dve", _tt(osv, ov[:, :, 0:64], rdn.rearrange("p (h o) -> p h o", o=1).to_broadcast([128, 4, 64]), ALU.mult),
                         reads=[T_bank[ob]] + ts(22, 1), writes=ts(21, 1))

                def att_transpose(ti):
                    b = k.nb()
                    fns = [_tr(bankb(b)[:, c * 128:(c + 1) * 128], o_sb[:, c * 128:(c + 1) * 128], identb[:]) for c in range(8)]
                    k.group(fns, reads=ts(21, 1) + [T_const], writes=[T_bank[b]])
                    k.op("act", _act(yaT[:, :, ti * 128:(ti + 1) * 128], bankb(b).rearrange("p (c t) -> p c t", c=8), AF.Copy),
                         reads=[T_bank[b]], writes=[T_ya[c][ti] for c in range(8)], extra=fence_deps(sum(T_pre[6:9], [])))

                agen = att_gen(len(ptiles))
                adone = [False]

                def att_step():
                    if not adone[0]:
                        try:
                            next(agen)
                        except StopIteration:
                            adone[0] = True

                if os.environ.get("KNOINTER"):
                    while not adone[0]:
                        att_step()
                if has_s:
                    lt = len(ptiles) - 1
                    k.op("act", _act(kprevA[:], kTA[:, :, lt * 128:(lt + 1) * 128], AF.Copy), reads=[T_kT[0][nt_of(lt)], T_kT[1][nt_of(lt)]], writes=[T_kprev])
                    k.op("act", _act(kprevB[:], kTB[:, :, lt * 128:(lt + 1) * 128], AF.Copy), reads=[T_kT[0][nt_of(lt)], T_kT[1][nt_of(lt)]], writes=[T_kprev])
                    k.op("act", _act(vprev[:], vaug[:, lt, :, :], AF.Copy), reads=[T_v[lt]], writes=[T_kprev])
                    _chk("a2p")
                    ti = ntl - 1
                    k.reserved = {4, 5, 6, 7}
                    k.bank_rr = 0
                    msc = msc_v.rearrange("p (h q) -> p h q", h=NH)
                    SB2 = [(8, 9, 10, 11, 12), (0, 1, 2, 3, 5)]
                    cvaug2 = [cvaug_flat.rearrange("p (b g d) -> p b g d", b=4, g=4),
                              ycs[:, 7440:8480].rearrange("p (b g d) -> p b g d", b=4, g=4)]
                    T_cva = [T_ys_v, Trk()]
                    T_ys_extra.append(T_cva[1])
                    Ppad = Ppad_flat.rearrange("p (b h s) -> p b h s", b=4, h=NH)
                    k.op("dve", _ms(Ppad_flat, 0.0), writes=[T_ys_p], extra=[T_ys_m.w])
                    for i_ in range(2):
                        k.op("dve", _ms(cvaug2[i_][:, :, :, 64:65], 1.0), writes=[T_cva[i_]], extra=[T_ys_m.w])

                    def samp_A(sg):
                        sl = SB2[sg % 2]
                        ckg, cvg = scr_bf(sl[0], 1), scr_bf(sl[1], 1)
                        kcA = scr_bf(sl[2], 1).rearrange("p (b j s) -> p b j s", b=4, j=2)
                        kcB = scr_bf(sl[3], 1).rearrange("p (b j s) -> p b j s", b=4, j=2)
                        cva = cvaug2[sg % 2]
                        k.dma("pool", ckg.rearrange("p (b e) -> p b e", b=4), ck_d[sg * 4:(sg + 1) * 4].rearrange("b s e -> s b e"), writes=ts(sl[0], 1))
                        k.dma("pool", cvg.rearrange("p (b e) -> p b e", b=4), cv_d[sg * 4:(sg + 1) * 4].rearrange("b s e -> s b e"), writes=ts(sl[1], 1))
                        b = k.nb()
                        fns = [_tr(bankb(b)[:, (bl * 2 + j) * 128:(bl * 2 + j + 1) * 128], ckg[:, bl * 256 + j * 128:bl * 256 + (j + 1) * 128], identb[:])
                               for bl in range(4) for j in range(2)]
                        k.group(fns, reads=ts(sl[0], 1) + [T_const], writes=[T_bank[b]])
                        Tv = bankb(b).rearrange("p (b j s) -> p b j s", b=4, j=2)
                        k.op("act", _act(kcA[:], Tv[:], AF.Copy), reads=[T_bank[b]], writes=ts(sl[2], 1), hold=[b])
                        k.op("dve", _cp(kcB[0:64], Tv[64:128]), reads=[T_bank[b]], writes=ts(sl[3], 1), hold=[b])
                        k.op("act", _act(kcB[64:128], Tv[0:64], AF.Copy), reads=[T_bank[b]], writes=ts(sl[3], 1))
                        k.op("dve", _cp(cva[:, :, :, 0:64], cvg.rearrange("p (b g d) -> p b g d", b=4, g=4)), reads=ts(sl[1], 1), writes=[T_cva[sg % 2]])

                    def samp_B(sg):
                        sl = SB2[sg % 2]
                        kcA = scr_bf(sl[2], 1).rearrange("p (b j s) -> p b j s", b=4, j=2)
                        kcB = scr_bf(sl[3], 1).rearrange("p (b j s) -> p b j s", b=4, j=2)
                        Ec = scr_f32(sl[4], 1).rearrange("p (a b h q) -> p a b h q", a=2, b=4, h=8)
                        cva = cvaug2[sg % 2]
                        sbk = []
                        for hp in range(2):
                            b = k.nb()
                            Sv = bank(b)[:, 0:256].rearrange("p (b h q) -> p b h q", b=4, h=8)
                            fns = []
                            for bl in range(4):
                                sq = sg * 4 + bl
                                c0 = ti * 128 + sq * 8
                                for g in range(4):
                                    useA = (g % 2) == hp
                                    kt = (kcA if useA else kcB)[hp * 64:(hp + 1) * 64, bl, g // 2, :]
                                    qt = qT[hp * 64:(hp + 1) * 64, 2 * g:2 * g + 2, c0:c0 + 8]
                                    fns.append(_mm(Sv[:, bl, 2 * g:2 * g + 2, :], kt, qt, True, True))
                            k.group(fns, reads=ts(sl[2], 1) + ts(sl[3], 1) + [T_q[c][nt_of(ti)] for c in range(8)], writes=[T_bank[b]])
                            sbk.append(b)
                        for hp in range(2):
                            k.op("act", _act(Ec[:, hp], bank(sbk[hp])[:, 0:256].rearrange("p (b h q) -> p b h q", b=4, h=8), AF.Exp, scale=0.125),
                                 reads=[T_bank[sbk[hp]]], writes=ts(sl[4], 1))
                        for bl in range(4):
                            k.op("dve", _tt(Ppad[:, bl, :, bl * 8:(bl + 1) * 8].rearrange("p (a h) q -> p a h q", a=2), Ec[:, :, bl, :, :],
                                            msc.rearrange("p (a h) q -> p a h q", a=2), ALU.mult),
                                 reads=ts(sl[4], 1) + [T_ys_m], writes=[T_ys_p])
                        for g in range(4):
                            fns = []
                            for kk in range(2):
                                for hp in range(2):
                                    h = 4 * g + 2 * kk + hp
                                    hq = hp * 8 + g * 2 + kk
                                    hh = h - 4 * g
                                    for bl in range(4):
                                        fns.append(_mm(bank(4 + g)[32 * sg:32 * sg + 32, hh * 65:(hh + 1) * 65], Ppad[:, bl, hq, :], cva[:, bl, g, :],
                                                       False, (bl == 3 and kk == 1 and hp == 1), tp=(0, 32 * sg)))
                            k.group(fns, reads=[T_ys_p, T_cva[sg % 2]], writes=[T_bank[4 + g]])

                    samp_A(0)
                    samp_A(1)
                    att_tile(ti, True)
                    samp_B(0)
                    samp_A(2)
                    samp_B(1)
                    samp_A(3)
                    samp_B(2)
                    samp_B(3)
                    for g in range(4):
                        att_finish_group(ti, g, 4 + g)
                    k.reserved = set()
                    att_transpose(ti)

                _chk("a2")
                for c in range(8):
                    slot = wload(wa_d[12 + 4 * c:16 + 4 * c].rearrange("b p e -> p b e"), 8192, 4)
                    ub = c % 2
                    u = scr_f32(3 * ub, 3)[:, 0:2 + TP]
                    k.op("dve", _cp(u[:, 0:2], uprev[:, c, :]), reads=[T_uprev[c]], writes=ts(3 * ub, 3))
                    if has_s:
                        us = scr_f32(14, 1)[:, 0:160].rearrange("p (b t) -> p b t", b=NSEQ)
                        k.op("dve", _cp(us[:, :, 0:2], scT[:, c, :].rearrange("p (b r) -> p b r", b=NSEQ)), reads=[T_const], writes=ts(14, 1))
                    for nti in range(nnt):
                        o, n = ntiles_n[nti]
                        st_ = 6 + 4 * (nti % 2)
                        gcs, szs, t1, t2 = (scr_f32(st_ + i, 1)[:, 0:n] for i in range(4))
                        bgc, _, _ = proj_group(slot, 0, nti)
                        k.op("act", _act(gcs, bank(bgc)[:, 0:n], AF.Copy), reads=[T_bank[bgc]], writes=ts(st_, 1))
                        att_step()
                        bzc, _, _ = proj_group(slot, 3, nti)
                        k.op("act", _act(szs, bank(bzc)[:, 0:n], AF.Silu), reads=[T_bank[bzc]], writes=ts(st_ + 1, 1))
                        att_step()
                        bh, _, _ = proj_group(slot, 1, nti)
                        samp = has_s and o >= TP
                        if not samp:
                            uo = u[:, 2 + o:2 + o + n]
                            k.op("dve", _tt(uo, gcs, bank(bh)[:, 0:n], ALU.mult), reads=ts(st_, 1) + [T_bank[bh]], writes=ts(3 * ub, 3))
                            u0, u1, u2 = u[:, o:o + n], u[:, o + 1:o + 1 + n], uo
                            tt1 = t1
                            urd = ts(3 * ub, 3)
                        else:
                            uo = us[:, :, 2:10]
                            k.op("dve", _tt(uo, gcs.rearrange("p (b t) -> p b t", b=NSEQ), bank(bh)[:, 0:n].rearrange("p (b t) -> p b t", b=NSEQ), ALU.mult),
                                 reads=ts(st_, 1) + [T_bank[bh]], writes=ts(14, 1))
                            u0, u1, u2 = us[:, :, 0:8], us[:, :, 1:9], uo
                            tt1 = t1.rearrange("p (b t) -> p b t", b=NSEQ)
                            urd = ts(14, 1)
                        att_step()
                        bgb, _, _ = proj_group(slot, 2, nti)
                        k.op("dve", _tt(t2, szs, bank(bgb)[:, 0:n], ALU.mult), reads=ts(st_ + 1, 1) + [T_bank[bgb]], writes=ts(st_ + 3, 1))
                        att_step()
                        k.op("dve", _ts(tt1, u0, cwt[:, c, 0:1], ALU.mult), reads=urd + [T_const], writes=ts(st_ + 2, 1))
                        k.op("dve", _stt(tt1, u1, cwt[:, c, 1:2], tt1, ALU.mult, ALU.add), reads=urd + ts(st_ + 2, 1), writes=ts(st_ + 2, 1))
                        k.op("dve", _stt(tt1, u2, cwt[:, c, 2:3], tt1, ALU.mult, ALU.add), reads=urd + ts(st_ + 2, 1), writes=ts(st_ + 2, 1))
                        k.op("dve", _tt(ycT[:, c, o:o + n], t1, t2, ALU.mult), reads=ts(st_ + 2, 2), writes=[T_yc[c][nti], T_ys_m, T_ys_p, T_ys_v] + T_ys_extra, extra=fence_deps(sum(T_pre[4:7], [])))
                    k.op("act", _act(uprev[:, c, :], u[:, TP:TP + 2], AF.Copy), reads=ts(3 * ub, 3), writes=[T_uprev[c]])
                    if has_s:
                        k.op("act", _act(csall[:, c, 0:32].rearrange("p (b r) -> p b r", b=NSEQ), us[:, :, 8:10], AF.Copy), reads=ts(14, 1), writes=[T_cs[c]])
                    if bi == len(BLOCKS) - 1:
                        k.op("act", _act(csall[:, c, 32:34], u[:, TP:TP + 2], AF.Copy), reads=ts(3 * ub, 3), writes=[T_cs[c]])

                while not adone[0]:
                    att_step()
                _chk("a3")
                for zh in range(2):
                    slot = wload(wa_d[44 + zh * 4:48 + zh * 4].rearrange("b p e -> p b e"), 8192, 4)
                    for bq in range(4):
                        c = zh * 4 + bq
                        for nti in range(nnt):
                            b, o, n = proj_group(slot, bq, nti)
                            st_ = 6 + (nti % 2)
                            sz = scr_f32(st_, 1)[:, 0:n]
                            k.op("act", _act(sz, bank(b)[:, 0:n], AF.Silu), reads=[T_bank[b]], writes=ts(st_, 1))
                            tl = list(range(o // 128, (o + n) // 128))
                            k.op("dve", _tt(yaT[:, c, o:o + n], sz, yaT[:, c, o:o + n], ALU.mult),
                                 reads=ts(st_, 1), writes=[T_ya[c][t_] for t_ in tl])

                _chk("a4")
                for e in range(16):
                    slot = wload(wb_d[e], 6144)
                    wv = wring[:, slot, 0:6144].rearrange("p (k j) -> p k j", j=128)
                    for nti in range(nnt):
                        o, n = ntiles_n[nti]
                        tl = list(range(o // 128, (o + n) // 128))
                        bgc, bga, bac, baa = k.nb(), k.nb(), k.nb(), k.nb()
                        k.group([_mm(bank(bgc)[:, 0:n], wv[:, kc, :], xT[:, kc, o:o + n], kc == 0, kc == 15) for kc in range(16)],
                                reads=[T_w[slot], T_xT[nti]], writes=[T_bank[bgc]])
                        k.group([_mm(bank(bga)[:, 0:n], wv[:, 16 + kc, :], xT[:, kc, o:o + n], kc == 0, kc == 15) for kc in range(16)],
                                reads=[T_w[slot], T_xT[nti]], writes=[T_bank[bga]])
                        k.group([_mm(bank(bac)[:, 0:n], wv[:, 32 + kc, :], ycT[:, kc, o:o + n], kc == 0, kc == 7) for kc in range(8)],
                                reads=[T_w[slot]] + [T_yc[c][nti] for c in range(8)], writes=[T_bank[bac]])
                        k.group([_mm(bank(baa)[:, 0:n], wv[:, 40 + kc, :], yaT[:, kc, o:o + n], kc == 0, kc == 7) for kc in range(8)],
                                reads=[T_w[slot]] + [T_ya[c][t_] for c in range(8) for t_ in tl], writes=[T_bank[baa]])
                        st_ = 4 * (nti % 2)
                        sgc, sga, t1, t2 = (scr_f32(st_ + i, 1)[:, 0:n] for i in range(4))
                        k.op("act", _act(sgc, bank(bgc)[:, 0:n], AF.Sigmoid), reads=[T_bank[bgc]], writes=ts(st_, 1))
                        k.op("act", _act(sga, bank(bga)[:, 0:n], AF.Sigmoid), reads=[T_bank[bga]], writes=ts(st_ + 1, 1))
                        k.op("dve", _tt(t1, sgc, bank(bac)[:, 0:n], ALU.mult), reads=ts(st_, 1) + [T_bank[bac]], writes=ts(st_ + 2, 1))
                        k.op("dve", _tt(t2, sga, bank(baa)[:, 0:n], ALU.mult), reads=ts(st_ + 1, 1) + [T_bank[baa]], writes=ts(st_ + 3, 1))
                        wr = [T_m[e][nti]] + (G2 if (e == 0 and nti == 0) else [])
                        k.op("dve", _tt(mT[:, e, o:o + n], t1, t2, ALU.add), reads=ts(st_ + 2, 2), writes=wr)

                _chk("b")
                stt_ = scr_f32(4, 1)
                T_lnst = [[Trk() for _ in range(4)] for _ in range(9)]
                T_lnmv = [Trk() for _ in range(9)]
                pend_bn = [None]

                def ln_stats(ti, cc, stt_=stt_, T_lnst=T_lnst):
                    st_ = stt_[:, ti * 48:ti * 48 + 24].rearrange("p (c s) -> p c s", c=4)
                    k.op("dve", lambda e_, o_=st_[:, cc, :], i_=pre[:, ti, cc * 512:(cc + 1) * 512]: e_.bn_stats(out=o_, in_=i_),
                         reads=[T_pre[ti][cc]] + ts(4, 1), writes=[T_lnst[ti][cc]])

                def ln_a(ti, stt_=stt_, T_lnst=T_lnst, T_lnmv=T_lnmv, ntl=ntl):
                    if ti < 0 or ti >= ntl:
                        return
                    st_ = stt_[:, ti * 48:ti * 48 + 24].rearrange("p (c s) -> p c s", c=4)
                    mv = stt_[:, ti * 48 + 32:ti * 48 + 34]
                    rs = stt_[:, ti * 48 + 40:ti * 48 + 42]
                    k.op("dve", lambda e_, o_=mv, i_=st_: e_.bn_aggr(out=o_, in_=i_), reads=T_lnst[ti] + ts(4, 1), writes=[T_lnmv[ti]])
                    k.op("act", _act(rs[:, 0:1], mv[:, 1:2], AF.Ln, bias=LN_EPS), reads=[T_lnmv[ti]] + ts(4, 1), writes=[T_lnmv[ti]])
                    k.op("act", _act(rs[:, 0:1], rs[:, 0:1], AF.Exp, scale=-0.5), reads=ts(4, 1), writes=[T_lnmv[ti]])
                    k.op("act", _act(rs[:, 1:2], mv[:, 0:1], AF.Identity, scale=rs[:, 0:1]), reads=ts(4, 1), writes=[T_lnmv[ti]])
                    k.op("act", lambda e_, o_=rs[:, 1:2]: e_.mul(o_, o_, -1.0), reads=ts(4, 1), writes=[T_lnmv[ti]])
                    k.op("act", _act(pre[:, ti, :], pre[:, ti, :], AF.Identity, scale=rs[:, 0:1], bias=rs[:, 1:2]),
                         reads=[T_lnmv[ti]], writes=T_pre[ti])

                def ln_b(ti, rows=rows, ntl=ntl):
                    if ti < 0 or ti >= ntl:
                        return
                    k.op("dve", _tt(pre[:, ti, :], pre[:, ti, :], gb[:, 0, :], ALU.mult), reads=[T_mg], writes=T_pre[ti])
                    k.op("dve", _tt(pre[:, ti, :], pre[:, ti, :], gb[:, 1, :], ALU.add), reads=[T_mg], writes=T_pre[ti])
                    k.dma("sp", y_d[rows[ti]:rows[ti] + 128, :], pre[:, ti, :], reads=T_pre[ti], final=True)

                k.dma("sp", mg[:, 0, :], g_d[:, :], writes=[T_mg])
                k.dma("sp", mg[:, 1, :], b_d[:, :], writes=[T_mg])
                xl_i = [0]
                ln_last = [None]

                def ln_ready(ti):
                    lnq.append((ti, lambda t=ti, f=ln_a: f(t)))
                    if ln_last[0] is not None:
                        lnq.append((ln_last[0], lambda t=ln_last[0], f=ln_b: f(t)))
                    ln_last[0] = ti

                LEAD = 4
                its = [(0, t) for t in range(ntl)] + [(1, t) for t in range(ntl)]
                for t in range(ntl + LEAD):
                    if t < ntl:
                        its.append((2, t))
                    if t - LEAD >= 0:
                        its.append((3, t - LEAD))
                XS = [0, 1, 2, 3, 5, 6, 7, 8]
                LA = 7

                def x32_load(i):
                    if i < len(its):
                        nb2, t2 = its[i]
                        k.dma("sp", scr_f32(XS[i % 8], 1), x_d[rows[t2]:rows[t2] + 128, nb2 * 512:(nb2 + 1) * 512], writes=ts(XS[i % 8], 1))

                for i in range(LA):
                    x32_load(i)
                wslot = {}
                for i, (nb_, ti) in enumerate(its):
                    if nb_ not in wslot:
                        wslot[nb_] = wload(wc_d[nb_], 8192)
                        if nb_ == 2:
                            wslot[3] = wload(wc_d[3], 8192)
                    slot = wslot[nb_]
                    wv = wring[:, slot, :].rearrange("p (k j) -> p k j", j=512)
                    x32_load(i + LA)
                    xs = XS[i % 8]
                    xl_i[0] += 1
                    x32 = scr_f32(xs, 1)
                    b = k.nb()
                    k.group([_mm(bank(b)[:, :], mT[:, kc, ti * 128:(ti + 1) * 128], wv[:, kc, :], kc == 0, kc == 15) for kc in range(16)],
                            reads=[T_w[slot]] + [T_m[e][nt_of(ti)] for e in range(16)], writes=[T_bank[b]])
                    wr = [T_pre[ti][nb_]] + (G1 if xl_i[0] == 1 else [])
                    k.op("dve", _stt(pre[:, ti, nb_ * 512:(nb_ + 1) * 512], x32, ALPHA, bank(b)[:, :], ALU.mult, ALU.add),
                         reads=ts(xs, 1) + [T_bank[b]], writes=wr)
                    if pend_bn[0] is not None:
                        ln_stats(*pend_bn[0])
                        if pend_bn[0][1] == 3:
                            ln_ready(pend_bn[0][0])
                    pend_bn[0] = (ti, nb_)
                    if lnq:
                        lnq.pop(0)[1]()
                ln_stats(*pend_bn[0])
                ln_ready(pend_bn[0][0])
                pend_bn[0] = None
                lnq.append((ln_last[0], lambda t=ln_last[0], f=ln_b: f(t)))
                if bi == len(BLOCKS) - 1:
                    while lnq:
                        lnq.pop(0)[1]()

            b = k.nb()
            fns = [_tr(bank(b)[0:34, c * 128:(c + 1) * 128], csall[:, c, :], identf[:]) for c in range(4)]
            k.group(fns, reads=T_cs + [T_const], writes=[T_bank[b]])
            b2 = k.nb()
            fns = [_tr(bank(b2)[0:34, (c - 4) * 128:(c - 3) * 128], csall[:, c, :], identf[:]) for c in range(4, 8)]
            k.group(fns, reads=T_cs + [T_const], writes=[T_bank[b2]])
            cso = scr_f32(8, 2)[0:34, :]
            k.op("dve", _cp(cso[:, 0:512], bank(b)[0:34, :]), reads=[T_bank[b]], writes=ts(8, 2))
            k.op("dve", _cp(cso[:, 512:1024], bank(b2)[0:34, :]), reads=[T_bank[b2]], writes=ts(8, 2))
            k.dma("sp", cst_d[:, :], cso, reads=ts(8, 2), final=True)
            k.st["sp"].add(lambda e: None, k.final, None)

        except _Stop:
            k.st["sp"].add(lambda e: None, k.final, None)
        with nc.Block() as block:
            block.tensor(lambda e: k.st["pe"].emit(e))
            block.scalar(lambda e: k.st["act"].emit(e))
            block.vector(lambda e: k.st["dve"].emit(e))
            block.gpsimd(lambda e: k.st["pool"].emit(e))
            block.sync(lambda e: k.st["sp"].emit(e))
    return nc


def _host_tables():
    h = np.arange(1, NH + 1, dtype=np.float64)
    slopes = 2.0 ** (-8.0 * h / NH)
    j = np.arange(128)[:, None, None].astype(np.float64)
    i = np.arange(128)[None, None, :].astype(np.float64)
    sl = slopes[None, :, None]
    mprev = np.where(j >= i, np.exp(-sl * (128 + i - j)), 0.0)
    mown = np.where(j <= i, np.exp(-sl * (i - j)), 0.0)
    bj, pj = np.divmod(np.arange(128), 8)
    same = (bj[:, None] == bj[None, :]) & (pj[:, None] <= pj[None, :])
    dl = (pj[None, :] - pj[:, None]).astype(np.float64)
    msnew = np.where(same[:, None, :], np.exp(-sl * dl[:, None, :]), 0.0)
    p = np.arange(8)[None, None, :].astype(np.float64)
    jj = np.arange(128)[:, None, None].astype(np.float64)
    msc_h = np.where(jj >= p, np.exp(-sl * (128 + p - jj)), 0.0)
    order = [4 * g + 2 * kk + hp for hp in range(2) for g in range(4) for kk in range(2)]
    msc = msc_h[:, order, :]
    f = lambda a: np.ascontiguousarray(a.reshape(128, -1)).astype(np.float32)
    return f(mprev), f(mown), f(msnew), f(msc)


_NC_CACHE = {}


def kernel(x_prompt, x_sample, cache_k, cache_v, state_conv, w_in, conv_w, attn_sinks,
           w_conv_out, w_attn_out, w_out, ln_g, ln_b):
    f32 = np.float32
    w_in = np.asarray(w_in, f32)[0]
    wco = np.asarray(w_conv_out, f32)[0]
    wao = np.asarray(w_attn_out, f32)[0]
    wo = np.asarray(w_out, f32)[0]

    def blk(w, c0, nk):
        return w[:, c0:c0 + 128].reshape(nk, 128, 128).transpose(1, 0, 2)

    cols = [4096 + 128 * c for c in range(8)] + [5120, 5248, 5376, 5504]
    for c in range(8):
        cols += [1024 + 128 * c, 2048 + 128 * c, 0 + 128 * c, 3072 + 128 * c]
    cols += [5632 + 128 * c for c in range(8)]
    wa = np.stack([blk(w_in, c0, 16) for c0 in cols]).reshape(52, 128, 16 * 128)
    wb = np.stack([np.concatenate([blk(w_in, 6656 + 128 * e, 16), blk(w_in, 8704 + 128 * e, 16),
                                   blk(wco, 128 * e, 8), blk(wao, 128 * e, 8)], axis=1) for e in range(16)]).reshape(16, 128, 48 * 128)
    wc = np.stack([wo[:, n * 512:(n + 1) * 512].reshape(16, 128, 512).transpose(1, 0, 2) for n in range(4)]).reshape(4, 128, 16 * 512)
    cw = np.ascontiguousarray(np.asarray(conv_w, f32)[0].reshape(3, 8, 128).transpose(2, 1, 0)).reshape(128, 24)
    sinks = np.ascontiguousarray(np.broadcast_to(np.asarray(attn_sinks, f32)[0][None, :], (128, 16)))
    lng = np.ascontiguousarray(np.broadcast_to(np.asarray(ln_g, f32)[0][None, :], (128, D)))
    lnb = np.ascontiguousarray(np.broadcast_to(np.asarray(ln_b, f32)[0][None, :], (128, D)))
    mprev, mown, msnew, msc = _host_tables()
    shared = {"wa": np.ascontiguousarray(wa), "wb": np.ascontiguousarray(wb), "wc": np.ascontiguousarray(wc),
              "cw": cw, "sinks": sinks, "lng": lng, "lnb": lnb, "ident": np.eye(128, dtype=f32),
              "mprev": mprev, "mown": mown, "msnew": msnew, "mscache": msc}
    xp = np.asarray(x_prompt, f32)
    xs = np.asarray(x_sample, f32)
    ck = np.asarray(cache_k, f32)[0].reshape(128, WIN, 256)
    cv = np.asarray(cache_v, f32)[0].reshape(128, WIN, 256)
    sc = np.asarray(state_conv, f32)[0]
    in_maps = []
    for c in range(NCORES):
        m = dict(shared)
        m["x"] = np.ascontiguousarray(np.concatenate([xp[c], xs[NSEQ * c:NSEQ * (c + 1)].reshape(NSEQ * DS, D)], axis=0))
        m["ck"] = np.ascontiguousarray(ck[NSEQ * c:NSEQ * (c + 1)])
        m["cv"] = np.ascontiguousarray(cv[NSEQ * c:NSEQ * (c + 1)])
        m["sc"] = np.ascontiguousarray(sc[NSEQ * c:NSEQ * (c + 1)].reshape(NSEQ * 2, DC))
        in_maps.append(m)
    if "nc" not in _NC_CACHE:
        _NC_CACHE["nc"] = build_nc()
    res = run_bass_kernel_spmd(_NC_CACHE["nc"], in_maps, core_ids=list(range(NCORES)))
    R = res.results
    y_prompt = np.stack([R[c]["y"][0:SEQ] for c in range(NCORES)]).astype(f32)
    y_sample = np.concatenate([R[c]["y"][SEQ:].reshape(NSEQ, DS, D) for c in range(NCORES)], axis=0).astype(f32)
    k_prompt = np.stack([R[c]["kvp"][:, 0:256].reshape(WIN, NKV, HD) for c in range(NCORES)])[None].astype(f32)
    v_prompt = np.stack([R[c]["kvp"][:, 256:512].reshape(WIN, NKV, HD) for c in range(NCORES)])[None].astype(f32)
    conv_prompt = np.stack([R[c]["cst"][32:34] for c in range(NCORES)])[None].astype(f32)
    k_sample = np.concatenate([R[c]["ks"].reshape(NSEQ, WIN, NKV, HD) for c in range(NCORES)], axis=0)[None].astype(f32)
    v_sample = np.concatenate([R[c]["vs"].reshape(NSEQ, WIN, NKV, HD) for c in range(NCORES)], axis=0)[None].astype(f32)
    conv_sample = np.concatenate([R[c]["cst"][0:32].reshape(NSEQ, 2, DC) for c in range(NCORES)], axis=0)[None].astype(f32)
    return (y_prompt, y_sample, k_prompt, v_prompt, conv_prompt, k_sample, v_sample, conv_sample)
```
